# Optimizing a Trainium2 kernel written in Bass

```python
import jax, jax.numpy as jnp
from jax import lax
import numpy as np

D_MODEL = 2048
BATCH = 16
SEQ = 2048
DEPTH = 2
DEC_BATCH = 1
DEC_SEQ = 8192
PAST_LEN = 128

N_EVEN = (DEPTH + 1) // 2
N_ODD = DEPTH // 2
CONV_DIM = D_MODEL // 2
CONV_WIDTH = 3
MLA_HEADS = 8
NOPE_DIM = 128
ROPE_DIM = 64
V_DIM = 128
QK_DIM = NOPE_DIM + ROPE_DIM
Q_RANK = D_MODEL // 4
KV_RANK = D_MODEL // 8
IN0_DIM = 3 * CONV_DIM + Q_RANK + KV_RANK + ROPE_DIM
MIX0_OUT = CONV_DIM + MLA_HEADS * V_DIM
POOL_WINDOWS = (2, 4, 8, 16)
POOL_GROUP = D_MODEL // len(POOL_WINDOWS)
D_FF = 4 * D_MODEL
ROPE_THETA = 10000.0
Q_BLOCK = 128
EPS = 1e-6

kernel_name = "hybrid_conv_mla_pool_encoder"


def rmsnorm(x, g):
    xf = x.astype(jnp.float32)
    y = xf * lax.rsqrt(jnp.mean(xf * xf, axis=-1, keepdims=True) + EPS)
    return (y * g.astype(jnp.float32)).astype(x.dtype)


def rope_tables(seq):
    inv = 1.0 / (ROPE_THETA ** (jnp.arange(0, ROPE_DIM, 2, dtype=jnp.float32) / ROPE_DIM))
    ang = jnp.arange(seq, dtype=jnp.float32)[:, None] * inv[None, :]
    return jnp.cos(ang), jnp.sin(ang)


def apply_rope(x, cos, sin):
    x1, x2 = jnp.split(x.astype(jnp.float32), 2, axis=-1)
    c = cos[None, :, None, :]
    s = sin[None, :, None, :]
    return jnp.concatenate([x1 * c - x2 * s, x1 * s + x2 * c], axis=-1).astype(x.dtype)


def short_conv_mixer(xb, xc, xi, conv_w):
    s = xi.shape[1]
    u = xc * xi
    up = jnp.pad(u, ((0, 0), (1, 1), (0, 0)))
    v = conv_w[0] * up[:, :s] + conv_w[1] * up[:, 1:s + 1] + conv_w[2] * up[:, 2:]
    return xb * v


def mla_mixer(cq, ckv, kr, q_a_norm, w_qb, kv_a_norm, w_kvb, q_norm, k_norm):
    bn, s, _ = cq.shape
    q = (rmsnorm(cq, q_a_norm) @ w_qb).reshape(bn, s, MLA_HEADS, QK_DIM)
    kv = (rmsnorm(ckv, kv_a_norm) @ w_kvb).reshape(bn, s, MLA_HEADS, NOPE_DIM + V_DIM)
    k_nope, v = kv[..., :NOPE_DIM], kv[..., NOPE_DIM:]
    k = jnp.concatenate([k_nope, jnp.broadcast_to(kr[:, :, None, :], (bn, s, MLA_HEADS, ROPE_DIM))], axis=-1)
    q = rmsnorm(q, q_norm)
    k = rmsnorm(k, k_norm)
    cos, sin = rope_tables(s)
    q = jnp.concatenate([q[..., :NOPE_DIM], apply_rope(q[..., NOPE_DIM:], cos, sin)], axis=-1)
    k = jnp.concatenate([k[..., :NOPE_DIM], apply_rope(k[..., NOPE_DIM:], cos, sin)], axis=-1)
    scale = QK_DIM ** -0.5
    nblk = s // Q_BLOCK
    qb = q.reshape(bn, nblk, Q_BLOCK, MLA_HEADS, QK_DIM).transpose(1, 0, 2, 3, 4)

    def block(qi):
        sc = jnp.einsum('bqhd,bkhd->bhqk', qi, k, preferred_element_type=jnp.float32) * scale
        p = jax.nn.softmax(sc, axis=-1)
        return jnp.einsum('bhqk,bkhd->bqhd', p.astype(v.dtype), v)

    o = lax.map(block, qb)
    return o.transpose(1, 0, 2, 3, 4).reshape(bn, s, MLA_HEADS * V_DIM)


def pool_mixer(h, w_pool, pool_scale):
    bn, s, d = h.shape
    hf = h.astype(jnp.float32)
    cs = jnp.concatenate([jnp.zeros((bn, 1, d), jnp.float32), jnp.cumsum(hf, axis=1)], axis=1)
    t = jnp.arange(s)
    outs = []
    for g, w in enumerate(POOL_WINDOWS):
        lo_c, hi_c = g * POOL_GROUP, (g + 1) * POOL_GROUP
        lo = jnp.clip(t - w // 2, 0, s)
        hi = jnp.clip(t + w // 2, 0, s)
        csg = cs[:, :, lo_c:hi_c]
        cnt = (hi - lo).astype(jnp.float32)[None, :, None]
        mean = (csg[:, hi] - csg[:, lo]) / cnt
        p = (mean - hf[:, :, lo_c:hi_c]).astype(h.dtype)
        outs.append(p @ w_pool[g])
    return jnp.concatenate(outs, axis=-1) * pool_scale


def trunk(x, norm_mix0, w_in0, conv_w, q_a_norm, w_qb, kv_a_norm, w_kvb, q_norm, k_norm, w_o0,
          norm_mix1, w_pool, pool_scale, norm_mlp, w_up, w_down):
    splits = [CONV_DIM, 2 * CONV_DIM, 3 * CONV_DIM, 3 * CONV_DIM + Q_RANK, 3 * CONV_DIM + Q_RANK + KV_RANK]
    for i in range(DEPTH):
        j = i // 2
        if i % 2 == 0:
            h = rmsnorm(x, norm_mix0[j])
            u = h @ w_in0[j]
            xb, xc, xi, cq, ckv, kr = jnp.split(u, splits, axis=-1)
            a_out = short_conv_mixer(xb, xc, xi, conv_w[j])
            b_out = mla_mixer(cq, ckv, kr, q_a_norm[j], w_qb[j], kv_a_norm[j], w_kvb[j], q_norm[j], k_norm[j])
            x = x + jnp.concatenate([a_out, b_out], axis=-1) @ w_o0[j]
        else:
            h = rmsnorm(x, norm_mix1[j])
            x = x + pool_mixer(h, w_pool[j], pool_scale[j])
        h = rmsnorm(x, norm_mlp[i])
        x = x + jnp.square(jax.nn.relu(h @ w_up[i])) @ w_down[i]
    return x


def setup_inputs(seed: int = 0) -> dict:
    key = jax.random.key(seed)
    ks = jax.random.split(key, 20)
    f32 = jnp.float32

    def nrm(k, shape, fan_in):
        return jax.random.normal(k, shape, f32) * (fan_in ** -0.5)

    def gain(k, shape):
        return 1.0 + 0.02 * jax.random.normal(k, shape, f32)

    return {
        "x_prompt": jax.random.normal(ks[0], (BATCH, SEQ, D_MODEL), f32),
        "x_sample": jax.random.normal(ks[1], (DEC_BATCH, DEC_SEQ, D_MODEL), f32),
        "norm_mix0": gain(ks[2], (N_EVEN, D_MODEL)),
        "w_in0": nrm(ks[3], (N_EVEN, D_MODEL, IN0_DIM), D_MODEL),
        "conv_w": nrm(ks[4], (N_EVEN, CONV_WIDTH, CONV_DIM), CONV_WIDTH),
        "q_a_norm": gain(ks[5], (N_EVEN, Q_RANK)),
        "w_qb": nrm(ks[6], (N_EVEN, Q_RANK, MLA_HEADS * QK_DIM), Q_RANK),
        "kv_a_norm": gain(ks[7], (N_EVEN, KV_RANK)),
        "w_kvb": nrm(ks[8], (N_EVEN, KV_RANK, MLA_HEADS * (NOPE_DIM + V_DIM)), KV_RANK),
        "q_norm": gain(ks[9], (N_EVEN, QK_DIM)),
        "k_norm": gain(ks[10], (N_EVEN, QK_DIM)),
        "w_o0": nrm(ks[11], (N_EVEN, MIX0_OUT, D_MODEL), MIX0_OUT),
        "norm_mix1": gain(ks[12], (N_ODD, D_MODEL)),
        "w_pool": nrm(ks[13], (N_ODD, len(POOL_WINDOWS), POOL_GROUP, POOL_GROUP), POOL_GROUP),
        "pool_scale": gain(ks[14], (N_ODD, D_MODEL)),
        "norm_mlp": gain(ks[15], (DEPTH, D_MODEL)),
        "w_up": nrm(ks[16], (DEPTH, D_MODEL, D_FF), D_MODEL),
        "w_down": nrm(ks[17], (DEPTH, D_FF, D_MODEL), D_FF),
    }


def reference(x_prompt, x_sample, norm_mix0, w_in0, conv_w, q_a_norm, w_qb, kv_a_norm, w_kvb, q_norm, k_norm,
              w_o0, norm_mix1, w_pool, pool_scale, norm_mlp, w_up, w_down):
    y_prompt = trunk(x_prompt, norm_mix0, w_in0, conv_w, q_a_norm, w_qb, kv_a_norm, w_kvb, q_norm, k_norm, w_o0,
                     norm_mix1, w_pool, pool_scale, norm_mlp, w_up, w_down)
    y_sample = trunk(x_sample, norm_mix0, w_in0, conv_w, q_a_norm, w_qb, kv_a_norm, w_kvb, q_norm, k_norm, w_o0,
                     norm_mix1, w_pool, pool_scale, norm_mlp, w_up, w_down)
    return (y_prompt, y_sample)
```

```python
import numpy as np
from contextlib import ExitStack
import concourse.bass as bass
import concourse.mybir as mybir
from concourse.bass_utils import run_bass_kernel_spmd

F32 = mybir.dt.float32
BF16 = mybir.dt.bfloat16
AF = mybir.ActivationFunctionType
ALU = mybir.AluOpType

D = 2048
NCH = 16
HALO = 12
TOWN = 512
W = TOWN + 2 * HALO
SP = 2048
SS = 8192
EPS = 1e-6
NSLOT = 3
SAME_ENG_SYNC = False
INTERLEAVE = False
SC_RING = [2, 3, 0]
POOL_OFFLOAD = False
ILV_SCHED = {1: 2, 4: 1, 6: 1, 8: 1, 10: 1, 11: 1, 13: 1}
SLOT_ELEMS = 4096
C_NM0, C_NM1, C_NMLP0, C_NMLP1, C_PSC, C_GQA, C_GKVA, C_GQN, C_GQR, C_GKN, C_GKR, C_CW = 0, 16, 32, 48, 64, 80, 84, 86, 87, 88, 89, 90
NCONST = 114


class Op:
    __slots__ = ("eng", "fn", "waits", "sig", "val", "dma")


class Sched:
    ENGS = ("pe", "act", "dve", "sync", "gq")

    def __init__(self):
        self.q = {e: [] for e in self.ENGS}
        self.lw = {}
        self.rd = {}
        self.dcount = {}
        self.nid = 0

    def op(self, eng, fn, r=(), w=(), dma=None):
        o = Op()
        o.eng = eng; o.fn = fn; o.sig = False; o.val = None; o.dma = dma
        if dma is not None:
            self.dcount[dma] = self.dcount.get(dma, 0) + 16
            o.val = self.dcount[dma]
        deps = []
        for k in r:
            p = self.lw.get(k)
            if p is not None:
                deps.append(p)
            if k[0] == "ps":
                d = self.rd.get(k)
                if d:
                    deps.extend(d.values())
        for k in w:
            p = self.lw.get(k)
            if p is not None:
                deps.append(p)
            d = self.rd.get(k)
            if d:
                deps.extend(d.values())
        waits = []
        seen = set()
        for p in deps:
            if id(p) in seen:
                continue
            seen.add(id(p))
            if p.dma is None and dma is None and p.eng == eng and (eng == "pe" or not SAME_ENG_SYNC):
                continue
            if p.dma is None:
                p.sig = True
            waits.append(p)
        o.waits = waits
        self.nid += 1
        rk = eng if dma is None else ("dma", self.nid)
        for k in r:
            self.rd.setdefault(k, {})[rk] = o
        for k in w:
            self.lw[k] = o
            self.rd[k] = {}
        self.q[eng].append(o)
        return o

    def finalize(self):
        for e in self.ENGS:
            n = 0
            for o in self.q[e]:
                if o.dma is None and o.sig:
                    n += 1
                    o.val = n

    def emit(self, eng, e, esem, dsem, final_waits=()):
        waited = {}

        def wait(p):
            if p.dma is None:
                sem, key = esem[p.eng], p.eng
            else:
                sem, key = dsem[p.dma], p.dma
            if waited.get(key, 0) < p.val:
                e.wait_ge(sem, p.val)
                waited[key] = p.val
        for o in self.q[eng]:
            for p in o.waits:
                wait(p)
            ins = o.fn(e)
            if o.dma is not None:
                ins.then_inc(dsem[o.dma], 16)
            elif o.sig:
                ins.then_inc(esem[eng], 1)
        for p in final_waits:
            wait(p)


def build_program(tiles, ctxs, stage=9):
    NT = len(tiles)
    nc = bass.Bass("TRN2", target_bir_lowering=False)

    def din(name, shape):
        return nc.dram_tensor(name, shape, F32, kind="ExternalInput").ap()
    xt_d = din("xt", [NT, W, D])
    xcp_d = din("xcp", [2, SP, D])
    xcs_d = din("xcs", [SS, D])
    w_in0 = din("w_in0", [D, 3904])
    w_qb = din("w_qb", [512, 1536])
    w_kvb = din("w_kvb", [256, 2048])
    w_o0 = din("w_o0", [D, D])
    w_pool = din("w_pool", [4, 512, 512])
    w_up = din("w_up", [2, D, 8192])
    w_down = din("w_down", [2, 8192, D])
    const_d = din("consts", [128, NCONST])
    ident_d = din("ident", [128, 128])
    ropec_d = din("ropec", [64, 2, SS])
    ropeq_d = din("ropeq", [NT, 64, 2, W])
    mask_d = din("mask", [NT, W])
    invt_d = din("invt", [NT, 4, W])
    y_d = nc.dram_tensor("y", [NT, TOWN, D], F32, kind="ExternalOutput").ap()

    S = Sched()
    with ExitStack() as es:
        def sb(name, shape, dt):
            return es.enter_context(nc.sbuf_tensor(name, shape, dt))
        xT = sb("xT", [128, NCH, W], F32)
        hT = sb("hT", [128, NCH * W], BF16)
        mh = sb("mh", [128, NCH, W], BF16)
        cqg = sb("cqg", [128, 4, W], BF16)
        ckv = sb("ckv", [128, 2, SS], BF16)
        KR = sb("KR", [128, SS], BF16)
        sskr = sb("sskr", [128, SS // 128], F32)
        rk = sb("rk", [128, 32], F32)
        sqK2 = sb("sqK2", [128, 512], BF16)
        stin = sb("stin", [128, D], F32)
        stout = sb("stout", [128, D], F32)
        ubuf = sb("ubuf", [128, W + 2], F32)
        T = [sb(f"T{i}", [128, W], F32) for i in range(6)]
        sq = [sb(f"sq{i}", [128, W], BF16) for i in range(2)]
        ropet = sb("ropet", [64, 2, W], F32)
        wqb = sb("wqb", [128, 4, 1600], BF16)
        ones_lo = sb("ones_lo", [128, 128], BF16)
        wkvb = sb("wkvb", [128, 2, 2048], BF16)
        slots = [sb(f"ws{i}", [128, SLOT_ELEMS], BF16) for i in range(NSLOT)]
        cst = sb("cst", [128, NCONST], F32)
        ident = sb("ident_sb", [128, 128], F32)
        ones = sb("ones", [128, 128], BF16)
        onesf = sb("onesf", [128, 128], F32)
        ps = [es.enter_context(nc.psum_tensor(f"ps{i}", [128, 2, 512], F32)) for i in range(4)]
        esem = {e: es.enter_context(nc.semaphore(f"s_{e}")) for e in ("pe", "act", "dve", "gq")}
        dnames = [f"w{i}" for i in range(NSLOT)] + ["stin", "stout", "cst", "tab", "res0", "res1", "idn", "msk", "inv0", "inv1"]
        dsem = {n: es.enter_context(nc.semaphore(f"d_{n}")) for n in dnames}
        block = es.enter_context(nc.Block())

        def hTc(c, wd=W):
            return hT[:, c * W:c * W + wd]
        Kseg = hT[:, 0:2048]
        Vseg = hT[:, 4 * W:4 * W + 2048].rearrange("p (c n) -> p c n", c=16)
        sqK = hT[:, 8 * W:8 * W + 512]
        Qn = hT[:, 9 * W:10 * W]
        Qr = hT[0:64, 10 * W:11 * W]
        Pt = [hT[:, (11 + i) * W:(12 + i) * W] for i in range(3)]
        K_KSEG = [("hT", c) for c in range(4)]
        K_VSEG = [("hT", c) for c in range(4, 8)]
        K_SQK = [("hT", 8)]
        K_Q = [("hT", 9), ("hT", 10)]
        K_PT = [[("hT", 11 + i)] for i in range(3)]

        def pk(p):
            return [("ps", 0, 0), ("ps", 0, "v")] if p == 0 else [("ps", p)]

        def b2(ap, wd):
            return ap.rearrange("p (b n) -> p b n", b=2)

        def cc(i, n=1):
            return cst[:, i:i + n]

        S.op("sync", lambda e: e.dma_start(out=cst[:], in_=const_d), w=[("cst",)], dma="cst")
        S.op("sync", lambda e: e.dma_start(out=ident[:], in_=ident_d), w=[("ident",)], dma="idn")
        S.op("dve", lambda e: e.memset(wqb[:, :, 1536:1600], 0.0), w=[("wqbpad",)])
        S.op("gq", lambda e: e.dma_start(out=wqb[:, :, 0:1536], in_=w_qb.rearrange("(c p) n -> p c n", p=128)), w=[("wqb",)], dma="res0")
        S.op("gq", lambda e: e.dma_start(out=wkvb[:], in_=w_kvb.rearrange("(c p) n -> p c n", p=128)), w=[("wkvb",)], dma="res1")

        def init_dve(e):
            e.memset(ones[:], 1.0)
            e.memset(ones_lo[0:64, :], 1.0)
            e.memset(ones_lo[64:128, :], 0.0)
            e.memset(onesf[:], 1.0)
            e.memset(hT[:], 0.0)
            for j in range(SS // 512):
                e.memset(KR[64:128, j * 512:(j + 1) * 512], 0.0)
            return e.memset(ubuf[:], 0.0)
        S.op("dve", init_dve, w=[("ones",), ("ubuf",)] + [("hT", c) for c in range(NCH)] + [("KR", kb) for kb in range(SS // 512)])

        wstate = {"n": 0}

        def wload(src_ap, kc, ncols):
            i = wstate["n"] % NSLOT
            wstate["n"] += 1
            view = slots[i][:, 0:kc * ncols].rearrange("p (c n) -> p c n", c=kc)
            S.op("gq", lambda e: e.dma_start(out=view, in_=src_ap), w=[("ws", i)], dma=f"w{i}")
            return view, ("ws", i)

        pstate = {"n": 0}

        def next_pair():
            p = pstate["n"] % 4
            pstate["n"] += 1
            return p

        evs = {"n": 0}

        def ev_eng():
            evs["n"] += 1
            return "act" if evs["n"] % 2 else "dve"

        def copy_op(eng, out, in_, r, w):
            if eng == "act":
                S.op("act", lambda e: e.activation(out=out, in_=in_, func=AF.Copy), r=r, w=w)
            else:
                S.op("dve", lambda e: e.tensor_copy(out=out, in_=in_), r=r, w=w)

        def mm(pair, lhs_list, rhs_list, wd, r, M=128, keys=None):
            bw = wd // 2
            nk = len(lhs_list)

            def fn(e):
                ins = None
                for k in range(nk):
                    for b in range(2):
                        ins = e.matmul(ps[pair][0:M, b, 0:bw], lhsT=lhs_list[k], rhs=rhs_list[k][:, b * bw:(b + 1) * bw],
                                       start=(k == 0), stop=(k == nk - 1))
                return ins
            S.op("pe", fn, r=r, w=keys if keys is not None else pk(pair))

        def psv(pair, wd, M=128):
            return ps[pair][0:M, :, 0:wd // 2]

        def load_x(src, wd):
            r0 = 0
            gi = 0
            while r0 < wd:
                R = min(128, wd - r0)
                if gi % 2 == 0:
                    stg, skeys, sdma = stin, [("stin",)], "stin"
                else:
                    stg, skeys, sdma = stout, [("stout", 0), ("stout", 1)], "stout"
                S.op("sync", lambda e, r0=r0, R=R, stg=stg: e.dma_start(out=stg[0:R, :], in_=src[r0:r0 + R, :]), w=skeys, dma=sdma)
                for half in range(2):
                    p = next_pair()
                    pv = ps[p][:].rearrange("p b (j n) -> p (b j) n", n=128)

                    def fn(e, R=R, half=half, pv=pv, stg=stg):
                        ins = None
                        for j in range(8):
                            c = half * 8 + j
                            ins = e.transpose(out=pv[:, j, 0:R], in_=stg[0:R, c * 128:(c + 1) * 128], identity=ident[0:R, 0:R])
                        return ins
                    S.op("pe", fn, r=skeys + [("ident",)], w=pk(p))
                    copy_op(ev_eng(), xT[:, half * 8:half * 8 + 8, r0:r0 + R], pv[:, 0:8, 0:R], r=pk(p),
                            w=[("xT", c) for c in range(half * 8, half * 8 + 8)])
                r0 += R
                gi += 1

        def rstd_from_ms(pair, wd, scale, dst, extra_r=()):
            def fn(e):
                e.activation(out=b2(dst[:, 0:wd], wd), in_=psv(pair, wd), func=AF.Ln, bias=EPS, scale=scale)
                return e.activation(out=dst[:, 0:wd], in_=dst[:, 0:wd], func=AF.Exp, scale=-0.5)
            S.op("act", fn, r=pk(pair) + list(extra_r), w=[("T", id(dst))])

        def tk(t):
            return ("T", id(t))

        def norm_stats(wd, dst):
            p = next_pair()
            bw = wd // 2
            for c in range(NCH):
                s_ = sq[c % 2]
                if c % 2 == 0:
                    S.op("act", lambda e, c=c, s_=s_: e.activation(out=s_[:, 0:wd], in_=xT[:, c, 0:wd], func=AF.Square),
                         r=[("xT", c)], w=[("sq", c % 2)])
                else:
                    S.op("dve", lambda e, c=c, s_=s_: e.tensor_tensor(out=s_[:, 0:wd], in0=xT[:, c, 0:wd], in1=xT[:, c, 0:wd], op=ALU.mult),
                         r=[("xT", c)], w=[("sq", c % 2)])

                def fn(e, c=c, s_=s_):
                    ins = None
                    for b in range(2):
                        ins = e.matmul(ps[p][:, b, 0:bw], lhsT=ones[:], rhs=s_[:, b * bw:(b + 1) * bw], start=(c == 0), stop=(c == NCH - 1))
                    return ins
                S.op("pe", fn, r=[("sq", c % 2), ("ones",)], w=pk(p))
            rstd_from_ms(p, wd, 1.0 / D, dst)

        def norm_to_hT(gcol, wd):
            rstd = T[5]
            norm_stats(wd, rstd)
            for c in range(NCH):
                S.op("dve", lambda e, c=c: e.scalar_tensor_tensor(out=hTc(c, wd), in0=xT[:, c, 0:wd], scalar=cc(gcol + c), in1=rstd[:, 0:wd],
                                                                  op0=ALU.mult, op1=ALU.mult),
                     r=[("xT", c), tk(rstd), ("cst",)], w=[("hT", c)])

        def rope(a, r1, r2, out, wd, rkeys, wkeys):
            cos = ropet[:, 0, 0:wd]
            sin = ropet[:, 1, 0:wd]

            def fn(e):
                e.tensor_tensor(out=r1[0:64, 0:wd], in0=a[0:64, 0:wd], in1=cos[0:64, :], op=ALU.mult)
                e.tensor_tensor(out=r2[0:32, 0:wd], in0=a[32:64, 0:wd], in1=sin[32:64, :], op=ALU.mult)
                e.tensor_tensor(out=r2[32:64, 0:wd], in0=a[0:32, 0:wd], in1=sin[0:32, :], op=ALU.mult)
                e.tensor_tensor(out=out[0:32, 0:wd], in0=r1[0:32, 0:wd], in1=r2[0:32, 0:wd], op=ALU.subtract)
                return e.tensor_tensor(out=out[32:64, 0:wd], in0=r1[32:64, 0:wd], in1=r2[32:64, 0:wd], op=ALU.add)
            S.op("dve", fn, r=[tk(a), ("ropet",)] + list(rkeys), w=[tk(r1), tk(r2)] + list(wkeys))

        def ctx_prep(src, Sk):
            nblk = Sk // 512
            for kb in range(nblk):
                wd = 512
                load_x(src[kb * 512:(kb + 1) * 512, :], wd)
                norm_to_hT(C_NM0, wd)
                S.op("sync", lambda e, kb=kb: e.dma_start(out=ropet[:, :, 0:512], in_=ropec_d[:, :, kb * 512:(kb + 1) * 512]),
                     w=[("ropet",)], dma="tab")
                wa, ka = wload(w_in0[:, 3584:3840].rearrange("(c p) n -> p c n", p=128), 16, 256)
                wb, kb_ = wload(w_in0[:, 3840:3904].rearrange("(c p) n -> p c n", p=128), 16, 64)
                hk = [("hT", c) for c in range(NCH)]
                rh = [hTc(c, wd) for c in range(NCH)]
                pm = [2, 3]
                for m in range(2):
                    mm(pm[m], [wa[:, k, m * 128:(m + 1) * 128] for k in range(NCH)], rh, wd, r=hk + [ka])
                    S.op("act", lambda e, m=m: e.activation(out=b2(sq[m][:, 0:wd], wd), in_=psv(pm[m], wd), func=AF.Square),
                         r=pk(pm[m]), w=[("sq", m)])

                def fn(e):
                    ins = None
                    for m in range(2):
                        for b in range(2):
                            ins = e.matmul(ps[1][:, b, 0:256], lhsT=ones[:], rhs=sq[m][:, b * 256:(b + 1) * 256], start=(m == 0), stop=(m == 1))
                    return ins
                S.op("pe", fn, r=[("sq", 0), ("sq", 1), ("ones",)], w=pk(1))
                rstd_from_ms(1, wd, 1.0 / 256, T[5])
                for m in range(2):
                    S.op("dve", lambda e, m=m, kb=kb: e.scalar_tensor_tensor(out=b2(ckv[:, m, kb * 512:(kb + 1) * 512], wd), in0=psv(pm[m], wd),
                                                                             scalar=cc(C_GKVA + m), in1=b2(T[5][:, 0:wd], wd), op0=ALU.mult, op1=ALU.mult),
                         r=pk(pm[m]) + [tk(T[5]), ("cst",)], w=[("ckv", kb)])
                mm(0, [wb[:, k, 0:64] for k in range(NCH)], rh, wd, r=hk + [kb_], M=64)
                S.op("act", lambda e: e.activation(out=b2(sq[0][0:64, 0:wd], wd), in_=psv(0, wd, 64), func=AF.Square), r=pk(0), w=[("sq", 0)])
                S.op("dve", lambda e: e.tensor_scalar(out=b2(T[4][0:64, 0:wd], wd), in0=psv(0, wd, 64), scalar1=cst[0:64, C_GKR:C_GKR + 1], scalar2=None, op0=ALU.mult),
                     r=pk(0) + [("cst",)], w=[tk(T[4])])

                def fn2(e):
                    ins = None
                    for j in range(4):
                        ins = e.matmul(ps[1][:, 1, 256 + j:257 + j], lhsT=sq[0][0:64, j * 128:(j + 1) * 128], rhs=ones[0:64, 0:1], start=True, stop=True)
                    return ins
                S.op("pe", fn2, r=[("sq", 0), ("ones",)], w=pk(1))
                S.op("dve", lambda e, kb=kb: e.tensor_copy(out=sskr[:, kb * 4:(kb + 1) * 4], in_=ps[1][:, 1, 256:260]), r=pk(1), w=[("sskr", kb)])
                rope(T[4], T[3], T[2], KR[0:64, kb * 512:(kb + 1) * 512], wd, [], [("KR", kb)])

        def in_proj_conv():
            hk = [("hT", c) for c in range(NCH)]
            rh = [hTc(c) for c in range(NCH)]
            for j in range(8):
                views = []
                for base in (1024, 2048, 0):
                    c0 = base + j * 128
                    views.append(wload(w_in0[:, c0:c0 + 128].rearrange("(c p) n -> p c n", p=128), 16, 128))
                (wxc, kxc), (wxi, kxi), (wxb, kxb) = views
                p1, p2, p3 = next_pair(), next_pair(), next_pair()
                mm(p1, [wxc[:, k, :] for k in range(NCH)], rh, W, r=hk + [kxc])
                S.op("act", lambda e, p1=p1: e.activation(out=b2(T[1][:], W), in_=psv(p1, W), func=AF.Copy), r=pk(p1), w=[tk(T[1])])
                mm(p2, [wxi[:, k, :] for k in range(NCH)], rh, W, r=hk + [kxi])
                S.op("act", lambda e, p2=p2: e.activation(out=b2(ubuf[:, 1:W + 1], W), in_=psv(p2, W), func=AF.Copy), r=pk(p2), w=[("ubuf",)])
                mm(p3, [wxb[:, k, :] for k in range(NCH)], rh, W, r=hk + [kxb])

                def fn(e, j=j, p3=p3):
                    cw = C_CW + 3 * j
                    e.tensor_tensor(out=ubuf[:, 1:W + 1], in0=T[1][:], in1=ubuf[:, 1:W + 1], op=ALU.mult)
                    e.tensor_scalar(out=T[3][:], in0=ubuf[:, 1:W + 1], scalar1=cc(cw + 1), scalar2=None, op0=ALU.mult)
                    e.scalar_tensor_tensor(out=T[3][:], in0=ubuf[:, 0:W], scalar=cc(cw), in1=T[3][:], op0=ALU.mult, op1=ALU.add)
                    e.scalar_tensor_tensor(out=T[3][:], in0=ubuf[:, 2:W + 2], scalar=cc(cw + 2), in1=T[3][:], op0=ALU.mult, op1=ALU.add)
                    return e.tensor_tensor(out=b2(mh[:, j, :], W), in0=psv(p3, W), in1=b2(T[3][:], W), op=ALU.mult)
                S.op("dve", fn, r=[tk(T[1]), ("ubuf",), ("cst",)] + pk(p3), w=[("ubuf",), tk(T[3]), ("mh", j)])
            pms = next_pair()
            for half in range(2):
                wq, kq = wload(w_in0[:, 3072 + half * 256:3072 + (half + 1) * 256].rearrange("(c p) n -> p c n", p=128), 16, 256)
                for mi in range(2):
                    m = half * 2 + mi
                    p = next_pair()
                    if p == pms:
                        p = next_pair()
                    mm(p, [wq[:, k, mi * 128:(mi + 1) * 128] for k in range(NCH)], rh, W, r=hk + [kq])
                    S.op("act", lambda e, p=p, m=m: e.activation(out=b2(sq[m % 2][:], W), in_=psv(p, W), func=AF.Square), r=pk(p), w=[("sq", m % 2)])
                    S.op("dve", lambda e, p=p, m=m: e.tensor_scalar(out=b2(cqg[:, m, :], W), in0=psv(p, W), scalar1=cc(C_GQA + m), scalar2=None, op0=ALU.mult),
                         r=pk(p) + [("cst",)], w=[("cqg", m)])

                    def fn(e, m=m):
                        ins = None
                        for b in range(2):
                            ins = e.matmul(ps[pms][:, b, 0:W // 2], lhsT=ones[:], rhs=sq[m % 2][:, b * (W // 2):(b + 1) * (W // 2)], start=(m == 0), stop=(m == 3))
                        return ins
                    S.op("pe", fn, r=[("sq", m % 2), ("ones",)], w=pk(pms))
            rstd_from_ms(pms, W, 1.0 / 512, T[2])

        def attention(Sk):
            bw = W // 2
            nseg = Sk // 2048
            scq = T[2]
            acc, rden, f, a_, r2 = T[0], T[1], T[3], T[4], T[5]
            stb = stout[:].bitcast(BF16)
            KB = [Kseg, stb[:, 0:2048]]
            VB = [Vseg, stb[:, 2048:4096].rearrange("p (c n) -> p c n", c=16)]
            kKB = [K_KSEG, [("stout", 0)]]
            kVB = [K_VSEG, [("stout", 1)]]
            SQB = [sqK, sqK2[:, :]]
            kSQB = [K_SQK, [("sqK2",)]]
            RKB = [rk[:, 0:16], rk[:, 16:32]]
            kRKB = [[("rk", 0)], [("rk", 1)]]
            QNB = [Qn, hT[:, 14 * W:15 * W]]
            QRB = [Qr, hT[0:64, 15 * W:16 * W]]
            QRF = [hT[:, 10 * W:11 * W], hT[:, 15 * W:16 * W]]

            def zq(e):
                e.memset(hT[64:128, 10 * W:11 * W], 0.0)
                return e.memset(hT[64:128, 15 * W:16 * W], 0.0)
            S.op("dve", zq, w=[("hT", 10), ("hT", 15)])
            kQB = [K_Q, [("hT", 14), ("hT", 15)]]
            units = [(h, sg) for h in range(8) for sg in range(nseg)]
            P0 = [("ps", 0, 0), ("ps", 0, "v")]

            def prep(u):
                h, sg = units[u]
                par = u % 2
                k0 = sg * 2048
                Kb, Vb, sqb, rkb = KB[par], VB[par], SQB[par], RKB[par]
                kK, kV, kS, kR = kKB[par], kVB[par], kSQB[par], kRKB[par]
                BK = [("ps", 0, 0), ("ps", 0, "v")]
                S.op("dve", lambda e: e.memset(rkb, 0.0), w=kR)
                for kb in range(4):
                    gkb = (k0 + kb * 512) // 512
                    bank = kb % 2

                    def fnk(e, kb=kb, bank=bank):
                        ins = None
                        for r in range(2):
                            ins = e.matmul(ps[0][:, bank, 0:512], lhsT=wkvb[:, r, h * 256:h * 256 + 128], rhs=ckv[:, r, k0 + kb * 512:k0 + (kb + 1) * 512],
                                           start=(r == 0), stop=(r == 1))
                        return ins
                    S.op("pe", fnk, r=[("wkvb",), ("ckv", gkb)], w=[BK[bank]])
                    S.op("act", lambda e, kb=kb, bank=bank: e.activation(out=Kb[:, kb * 512:(kb + 1) * 512], in_=ps[0][:, bank, 0:512], func=AF.Copy, scale=cc(C_GKN)),
                         r=[BK[bank], ("cst",)], w=kK)
                yield
                for kc2 in range(8):
                    gkb = (k0 + kc2 * 256) // 512
                    bank = kc2 % 2

                    def fnv(e, kc2=kc2, bank=bank):
                        ins = None
                        for j in range(2):
                            ks = k0 + (kc2 * 2 + j) * 128
                            for r in range(2):
                                ins = e.matmul(ps[0][:, bank, j * 256:(j + 1) * 256], lhsT=ckv[:, r, ks:ks + 128], rhs=wkvb[:, r, h * 256:h * 256 + 256],
                                               start=(r == 0), stop=(r == 1))
                        return ins
                    S.op("pe", fnv, r=[("wkvb",), ("ckv", gkb)], w=[BK[bank]])

                    def fna(e, kc2=kc2, bank=bank):
                        ins = None
                        for j in range(2):
                            kc = kc2 * 2 + j
                            ins = e.activation(out=sqb[:, j * 128:(j + 1) * 128], in_=ps[0][:, bank, j * 256:j * 256 + 128], func=AF.Square,
                                               accum_out=rkb[:, kc:kc + 1])
                        return ins
                    S.op("act", fna, r=[BK[bank]] + kR, w=kS + kR)
                    S.op("dve", lambda e, kc2=kc2, bank=bank: e.tensor_copy(out=Vb[:, kc2 * 2:kc2 * 2 + 2, :],
                                                                            in_=ps[0][:, bank, :].rearrange("p (c n) -> p c n", c=2)[:, :, 128:256]),
                         r=[BK[bank]], w=kV)
                yield
                S.op("dve", lambda e: e.tensor_tensor(out=rkb, in0=rkb, in1=sskr[:, sg * 16:(sg + 1) * 16], op=ALU.add),
                     r=kR + [("sskr", sg * 4 + i) for i in range(4)], w=kR)
                S.op("act", lambda e: e.activation(out=rkb, in_=rkb, func=AF.Ln, bias=192.0 * EPS, scale=1.0), r=kR, w=kR)
                S.op("dve", lambda e: e.tensor_scalar(out=rkb, in0=rkb, scalar1=-0.5, scalar2=None, op0=ALU.mult), r=kR, w=kR)
                S.op("act", lambda e: e.activation(out=rkb, in_=rkb, func=AF.Exp), r=kR, w=kR)
                yield
                if sg != 0:
                    return
                qp = h % 2
                Qn_, Qr_, kQ = QNB[qp], QRB[qp], kQB[qp]
                rq = [cqg[:, r, :] for r in range(4)]
                kq = [("cqg", r) for r in range(4)] + [("wqb",)]
                mm(0, [wqb[:, r, h * 192:h * 192 + 128] for r in range(4)], rq, W, r=kq, keys=P0)
                S.op("act", lambda e: e.activation(out=b2(sq[0][:], W), in_=psv(0, W), func=AF.Square), r=P0, w=[("sq", 0)])
                S.op("dve", lambda e: e.tensor_copy(out=b2(ubuf[:, 0:W], W), in_=psv(0, W)), r=P0, w=[("ubuf",)])
                yield
                mm(0, [wqb[:, r, h * 192 + 128:h * 192 + 256] for r in range(4)], rq, W, r=kq + [("wqbpad",)], keys=P0)
                S.op("act", lambda e: e.activation(out=b2(sq[1][:], W), in_=psv(0, W), func=AF.Square), r=P0, w=[("sq", 1)])
                S.op("dve", lambda e: e.tensor_scalar(out=b2(a_[0:64, :], W), in0=psv(0, W, 64), scalar1=cst[0:64, C_GQR:C_GQR + 1], scalar2=None, op0=ALU.mult),
                     r=P0 + [("cst",)], w=[tk(a_)])
                yield

                def fnm(e):
                    ins = None
                    for b in range(2):
                        e.matmul(ps[0][:, b, 0:bw], lhsT=ones[:], rhs=sq[0][:, b * bw:(b + 1) * bw], start=True, stop=False)
                        ins = e.matmul(ps[0][:, b, 0:bw], lhsT=ones_lo[:], rhs=sq[1][:, b * bw:(b + 1) * bw], start=False, stop=True)
                    return ins
                S.op("pe", fnm, r=[("sq", 0), ("sq", 1), ("ones",)], w=P0)

                def fnf(e):
                    e.tensor_tensor(out=b2(f[:], W), in0=psv(0, W), in1=b2(scq[:], W), op=ALU.mult)
                    return e.tensor_tensor(out=f[:], in0=f[:], in1=scq[:], op=ALU.mult)
                S.op("dve", fnf, r=P0 + [tk(scq)], w=[tk(f)])
                def fnl(e):
                    e.activation(out=f[:], in_=f[:], func=AF.Ln, bias=EPS, scale=1.0 / 192)
                    return e.activation(out=f[:], in_=f[:], func=AF.Exp, scale=-0.5)
                S.op("act", fnl, r=[tk(f)], w=[tk(f)])
                yield

                def fnq(e):
                    e.tensor_tensor(out=f[:], in0=f[:], in1=scq[:], op=ALU.mult)
                    e.scalar_tensor_tensor(out=Qn_, in0=ubuf[:, 0:W], scalar=cc(C_GQN), in1=f[:], op0=ALU.mult, op1=ALU.mult)
                    return e.tensor_tensor(out=a_[0:64, :], in0=a_[0:64, :], in1=f[0:64, :], op=ALU.mult)
                S.op("dve", fnq, r=[tk(f), tk(scq), tk(a_), ("ubuf",), ("cst",)], w=[tk(f), tk(a_)] + kQ)
                yield
                rope(a_, f, r2, Qr_, W, [], kQ)
                yield

            def drain(g, n=None):
                if g is None:
                    return
                k = 0
                for _ in g:
                    k += 1
                    if n is not None and k >= n:
                        return

            sc_i = {"n": 0}
            pt_i = {"n": 0}
            drain(prep(0))
            for u, (h, sg) in enumerate(units):
                par = u % 2
                qp = h % 2
                k0 = sg * 2048
                Kb, Vb, rkb = KB[par], VB[par], RKB[par]
                kK, kV, kR = kKB[par], kVB[par], kRKB[par]
                Qn_, Qr_, kQ = QNB[qp], QRF[qp], kQB[qp]
                nxt = prep(u + 1) if u + 1 < len(units) else None
                if not INTERLEAVE:
                    drain(nxt)
                    nxt = None
                prs = {}

                def scores(kc):
                    pr = SC_RING[sc_i["n"] % len(SC_RING)]
                    sc_i["n"] += 1
                    prs[kc] = pr
                    gkb = (k0 + kc * 128) // 512

                    def fn(e, kc=kc, pr=pr, Kb=Kb, Qn_=Qn_, Qr_=Qr_, k0=k0):
                        ins = None
                        for b in range(2):
                            e.matmul(ps[pr][:, b, 0:bw], lhsT=Kb[:, kc * 128:(kc + 1) * 128], rhs=Qn_[:, b * bw:(b + 1) * bw], start=True, stop=False)
                            ins = e.matmul(ps[pr][:, b, 0:bw], lhsT=KR[:, k0 + kc * 128:k0 + (kc + 1) * 128], rhs=Qr_[:, b * bw:(b + 1) * bw], start=False, stop=True)
                        return ins
                    S.op("pe", fn, r=kK + kQ + [("KR", gkb)], w=pk(pr))
                depth = len(SC_RING) - 1
                for k_ in range(depth):
                    scores(k_)
                for kc in range(16):
                    if kc + depth < 16:
                        scores(kc + depth)
                    pr = prs[kc]
                    pi = pt_i["n"] % 3
                    pt_i["n"] += 1
                    ptile = Pt[pi]
                    S.op("act", lambda e, kc=kc, pr=pr, ptile=ptile, rkb=rkb: e.activation(out=b2(ptile, W), in_=psv(pr, W), func=AF.Exp, scale=rkb[:, kc:kc + 1]),
                         r=pk(pr) + kR, w=K_PT[pi])
                    first = (sg == 0 and kc == 0)
                    last = (sg == nseg - 1 and kc == 15)

                    def fnpv(e, kc=kc, ptile=ptile, first=first, last=last, Vb=Vb):
                        ins = None
                        for b in range(2):
                            ins = e.matmul(ps[1][:, b, 0:bw], lhsT=Vb[:, kc, :], rhs=ptile[:, b * bw:(b + 1) * bw], start=first, stop=last)
                        return ins
                    S.op("pe", fnpv, r=kV + K_PT[pi], w=pk(1))
                    if first:
                        S.op("dve", lambda e, ptile=ptile: e.tensor_copy(out=acc[:], in_=ptile), r=K_PT[pi], w=[tk(acc)])
                    else:
                        S.op("dve", lambda e, ptile=ptile: e.tensor_tensor(out=acc[:], in0=acc[:], in1=ptile, op=ALU.add), r=K_PT[pi] + [tk(acc)], w=[tk(acc)])
                    if nxt is not None:
                        for _ in range(ILV_SCHED.get(kc, 0)):
                            next(nxt, None)
                drain(nxt)
                if sg == nseg - 1:
                    pd = SC_RING[sc_i["n"] % len(SC_RING)]
                    sc_i["n"] += 1

                    def fnd(e, pd=pd):
                        ins = None
                        for b in range(2):
                            ins = e.matmul(ps[pd][:, b, 0:bw], lhsT=onesf[:], rhs=acc[:, b * bw:(b + 1) * bw], start=True, stop=True)
                        return ins
                    S.op("pe", fnd, r=[tk(acc), ("ones",)], w=pk(pd))
                    def fnr_(e, pd=pd):
                        e.activation(out=b2(rden[:], W), in_=psv(pd, W), func=AF.Ln)
                        return e.activation(out=rden[:], in_=rden[:], func=AF.Exp, scale=-1.0)
                    S.op("act", fnr_, r=pk(pd), w=[tk(rden)])
                    S.op("dve", lambda e, h=h: e.tensor_tensor(out=b2(mh[:, 8 + h, :], W), in0=psv(1, W), in1=b2(rden[:], W), op=ALU.mult),
                         r=pk(1) + [tk(rden)], w=[("mh", 8 + h)])

        def resid_add(p, c):
            S.op("dve", lambda e: e.tensor_tensor(out=b2(xT[:, c, :], W), in0=psv(p, W), in1=b2(xT[:, c, :], W), op=ALU.add),
                 r=pk(p) + [("xT", c)], w=[("xT", c)])

        def out_proj():
            mk = [("mh", c) for c in range(NCH)]
            rm = [mh[:, c, :] for c in range(NCH)]
            for ch in range(8):
                wv, kk = wload(w_o0[:, ch * 256:(ch + 1) * 256].rearrange("(c p) n -> p c n", p=128), 16, 256)
                for mi in range(2):
                    p = next_pair()
                    mm(p, [wv[:, k, mi * 128:(mi + 1) * 128] for k in range(NCH)], rm, W, r=mk + [kk])
                    resid_add(p, ch * 2 + mi)

        def mlp(layer):
            norm_to_hT(C_NMLP0 if layer == 0 else C_NMLP1, W)
            hk = [("hT", c) for c in range(NCH)]
            rh = [hTc(c) for c in range(NCH)]
            for g in range(8):
                hs = (g % 2) * 8
                for q4 in range(4):
                    f0 = g * 1024 + q4 * 256
                    wv, kk = wload(w_up[layer, :, f0:f0 + 256].rearrange("(c p) n -> p c n", p=128), 16, 256)
                    for mi in range(2):
                        p = next_pair()
                        fc = q4 * 2 + mi
                        mm(p, [wv[:, k, mi * 128:(mi + 1) * 128] for k in range(NCH)], rh, W, r=hk + [kk])
                        tt = T[fc % 2]
                        S.op("act", lambda e, p=p, tt=tt: e.activation(out=b2(tt[:], W), in_=psv(p, W), func=AF.Relu), r=pk(p), w=[tk(tt)])
                        S.op("dve", lambda e, tt=tt, fc=fc, hs=hs: e.tensor_tensor(out=mh[:, hs + fc, :], in0=tt[:], in1=tt[:], op=ALU.mult),
                             r=[tk(tt)], w=[("mh", hs + fc)])
                mk = [("mh", hs + k) for k in range(8)]
                rm = [mh[:, hs + k, :] for k in range(8)]
                for dq in range(4):
                    wv, kk = wload(w_down[layer, g * 1024:(g + 1) * 1024, dq * 512:(dq + 1) * 512].rearrange("(c p) n -> p c n", p=128), 8, 512)
                    for dc in range(4):
                        p = next_pair()
                        mm(p, [wv[:, k, dc * 128:(dc + 1) * 128] for k in range(8)], rm, W, r=mk + [kk])
                        resid_add(p, dq * 4 + dc)

        def wsum(src, ta, tb, L, eng_fn_list):
            cur = src
            bufs = [ta, tb]
            lo, hi = 0, W
            for s in range(L):
                dst = bufs[s % 2]
                if s == 0:
                    nlo, nhi = lo + 1, hi
                    eng_fn_list.append(lambda e, cur=cur, dst=dst, nlo=nlo, nhi=nhi: e.tensor_tensor(
                        out=dst[:, nlo:nhi], in0=cur[:, nlo - 1:nhi - 1], in1=cur[:, nlo:nhi], op=ALU.add))
                else:
                    sh = 1 << (s - 1)
                    nlo, nhi = lo + sh, hi - sh
                    eng_fn_list.append(lambda e, cur=cur, dst=dst, nlo=nlo, nhi=nhi, sh=sh: e.tensor_tensor(
                        out=dst[:, nlo:nhi], in0=cur[:, nlo - sh:nhi - sh], in1=cur[:, nlo + sh:nhi + sh], op=ALU.add))
                lo, hi = nlo, nhi
                cur = dst
            return cur

        def pool_layer(ti):
            rstd = T[5]
            norm_stats(W, rstd)
            maskt = T[4]
            S.op("sync", lambda e: e.dma_start(out=maskt[:], in_=mask_d[ti:ti + 1, :].partition_broadcast(128)), w=[tk(maskt)], dma="msk")
            S.op("dve", lambda e: e.tensor_tensor(out=rstd[:], in0=rstd[:], in1=maskt[:], op=ALU.mult), r=[tk(rstd), tk(maskt)], w=[tk(rstd)])
            o0, o1 = HALO, HALO + TOWN
            wl = {}
            for gi in range(3):
                wl[gi] = wload(w_pool[gi].rearrange("(c p) n -> p c n", p=128), 4, 512)

            def make_inv(gi, ta, tb, inv, wkeys):
                S.op("sync", lambda e: e.dma_start(out=inv[:, 0:W], in_=invt_d[ti, gi:gi + 1, :].partition_broadcast(128)), w=wkeys, dma=f"inv{gi % 2}")

            def group_elementwise(gi, eng, h32, ta, tb, inv, tkeys):
                L = gi + 1
                for cq_ in range(4):
                    c = gi * 4 + cq_
                    if eng == "dve":
                        fl = [lambda e, c=c: e.scalar_tensor_tensor(out=h32[:, 0:W], in0=xT[:, c, :], scalar=cc(C_NM1 + c), in1=rstd[:], op0=ALU.mult, op1=ALU.mult)]
                    else:
                        fl = [lambda e, c=c: e.tensor_scalar(out=h32[:, 0:W], in0=xT[:, c, :], scalar1=cc(C_NM1 + c), scalar2=None, op0=ALU.mult),
                              lambda e: e.tensor_tensor(out=h32[:, 0:W], in0=h32[:, 0:W], in1=rstd[:], op=ALU.mult)]
                    s_ = wsum(h32, ta, tb, L, fl)
                    other = tb if s_ is ta else ta
                    fl.append(lambda e, s_=s_, other=other: e.tensor_tensor(out=other[:, o0:o1], in0=s_[:, o0:o1], in1=inv[:, o0:o1], op=ALU.mult))
                    fl.append(lambda e, c=c, other=other: e.tensor_tensor(out=hTc(c)[:, o0:o1], in0=other[:, o0:o1], in1=h32[:, o0:o1], op=ALU.subtract))

                    def run2(e, fl=fl):
                        ins = None
                        for f_ in fl:
                            ins = f_(e)
                        return ins
                    S.op(eng, run2, r=[("xT", c), tk(rstd), ("cst",)] + tkeys, w=tkeys + [("hT", c)])

            def group_matmul(gi, wv, kk):
                hk = [("hT", gi * 4 + k) for k in range(4)]
                rh = [hTc(gi * 4 + k) for k in range(4)]
                for m in range(4):
                    p = next_pair()
                    c = gi * 4 + m
                    mm(p, [wv[:, k, m * 128:(m + 1) * 128] for k in range(4)], rh, W, r=hk + [kk])
                    S.op("dve", lambda e, p=p, c=c: e.scalar_tensor_tensor(out=b2(xT[:, c, :], W), in0=psv(p, W), scalar=cc(C_PSC + c), in1=b2(xT[:, c, :], W),
                                                                           op0=ALU.mult, op1=ALU.add),
                         r=pk(p) + [("xT", c), ("cst",)], w=[("xT", c)])
            TK = [tk(T[0]), tk(T[1]), tk(T[2]), tk(T[3])]
            invb = [T[3], ubuf[:, 0:W]]
            invk = [[tk(T[3])], [("ubuf",)]]
            TK3 = [tk(T[0]), tk(T[1]), tk(T[2])]
            for gi in range(2):
                make_inv(gi, None, None, invb[gi % 2], invk[gi % 2])
            for gi in range(4):
                group_elementwise(gi, "dve", T[0], T[1], T[2], invb[gi % 2], TK3 + invk[gi % 2])
                if gi + 2 < 4:
                    make_inv(gi + 2, None, None, invb[gi % 2], invk[gi % 2])
                if gi < 3:
                    group_matmul(gi, *wl[gi])
            wv3, kk3 = wload(w_pool[3].rearrange("(c p) n -> p c n", p=128), 4, 512)
            group_matmul(3, wv3, kk3)

        stores = []

        def store(ti):
            for tg in range(4):
                col0 = HALO + tg * 128
                if tg % 2 == 0:
                    stg, skeys, sdma = stout, [("stout", 0), ("stout", 1)], "stout"
                else:
                    stg, skeys, sdma = stin, [("stin",), ("stin",)], "stin"
                for half in range(2):
                    p = next_pair()
                    pv = ps[p][:].rearrange("p b (j n) -> p (b j) n", n=128)

                    def fn(e, half=half, pv=pv, col0=col0):
                        ins = None
                        for j in range(8):
                            c = half * 8 + j
                            ins = e.transpose(out=pv[:, j, :], in_=xT[:, c, col0:col0 + 128], identity=ident[:, :])
                        return ins
                    S.op("pe", fn, r=[("xT", c) for c in range(half * 8, half * 8 + 8)] + [("ident",)], w=pk(p))
                    copy_op(ev_eng(), stg[:, half * 1024:(half + 1) * 1024].rearrange("p (j n) -> p j n", n=128), pv[:, 0:8, :], r=pk(p), w=[skeys[half]])
                o = S.op("sync", lambda e, tg=tg, stg=stg: e.dma_start(out=y_d[ti, tg * 128:(tg + 1) * 128, :], in_=stg[:, :]), r=skeys, dma=sdma)
                stores.append(o)

        ti = 0
        for ci, (kind, idx) in enumerate(ctxs):
            Sk = SP if kind == "p" else SS
            src = xcp_d[idx] if kind == "p" else xcs_d
            ctx_prep(src, Sk)
            while ti < NT and tiles[ti] == ci:
                load_x(xt_d[ti], W)
                S.op("sync", lambda e, ti=ti: e.dma_start(out=ropet[:, :, :], in_=ropeq_d[ti]), w=[("ropet",)], dma="tab")
                norm_to_hT(C_NM0, W)
                if stage == -2:
                    in_proj_conv()
                    attention(Sk)
                    S.op("dve", lambda e: e.tensor_copy(out=xT[:, 0, 12:28], in_=rk[:, :]), r=[("rk",)], w=[("xT", 0)])
                    S.op("dve", lambda e: e.tensor_copy(out=xT[:, 1, 12:28], in_=sskr[:, 0:16]), r=[("sskr", i) for i in range(4)], w=[("xT", 1)])
                if stage == -1:
                    in_proj_conv()
                    attention(Sk)
                    for c in range(NCH):
                        S.op("dve", lambda e, c=c: e.tensor_copy(out=xT[:, c, :], in_=mh[:, c, :]), r=[("mh", c)], w=[("xT", c)])
                if stage >= 1:
                    in_proj_conv()
                    attention(Sk)
                    out_proj()
                if stage >= 2:
                    mlp(0)
                if stage >= 3:
                    pool_layer(ti)
                if stage >= 4:
                    mlp(1)
                store(ti)
                ti += 1

        S.finalize()
        emap = {"pe": "tensor", "act": "scalar", "dve": "vector", "sync": "sync", "gq": "gpsimd"}

        @block.sync
        def _(e):
            S.emit("sync", e, esem, dsem, final_waits=stores[-2:])

        @block.gpsimd
        def _(e):
            S.emit("gq", e, esem, dsem)

        @block.tensor
        def _(e):
            S.emit("pe", e, esem, dsem)

        @block.scalar
        def _(e):
            S.emit("act", e, esem, dsem)

        @block.vector
        def _(e):
            S.emit("dve", e, esem, dsem)
    return nc


def _fm(g, nch):
    return np.ascontiguousarray(np.asarray(g, np.float32).reshape(nch, 128).T)


def _rope_table():
    inv = (1.0 / (np.float32(10000.0) ** (np.arange(0, 64, 2, dtype=np.float32) / np.float32(64)))).astype(np.float32)
    ang = np.arange(SS, dtype=np.float32)[:, None] * inv[None, :]
    cos = np.cos(ang).astype(np.float32).T
    sin = np.sin(ang).astype(np.float32).T
    tab = np.zeros((64, 2, SS), np.float32)
    tab[0:32, 0] = cos; tab[32:64, 0] = cos
    tab[0:32, 1] = sin; tab[32:64, 1] = sin
    return tab


TILES_FULL = [0, 0, 0, 0, 1, 1, 1, 1, 2, 2]
CTXS_FULL = [("p", 0), ("p", 1), ("s", 0)]


def make_inputs(core, tiles, ctxs, x_prompt, x_sample, common):
    NT = len(tiles)
    xt = np.zeros((NT, W, D), np.float32)
    ropeq = np.zeros((NT, 64, 2, W), np.float32)
    mask = np.zeros((NT, W), np.float32)
    invt = np.ones((NT, 4, W), np.float32)
    tab = common["ropec"]
    per_ctx = {}
    for t, ci in enumerate(tiles):
        k = per_ctx.get(ci, 0)
        per_ctx[ci] = k + 1
        kind, idx = ctxs[ci]
        if kind == "p":
            seq = x_prompt[2 * core + idx]
            Sk = SP
            start = k * TOWN
        else:
            seq = x_sample[0]
            Sk = SS
            start = core * 1024 + k * TOWN
        pos = np.arange(start - HALO, start + TOWN + HALO)
        valid = (pos >= 0) & (pos < Sk)
        pc = np.clip(pos, 0, Sk - 1)
        xt[t][valid] = seq[pos[valid]]
        ropeq[t] = tab[:, :, pc]
        mask[t] = valid.astype(np.float32)
        for gi, w_ in enumerate((2, 4, 8, 16)):
            lo_ = np.clip(pos - w_ // 2, 0, Sk)
            hi_ = np.clip(pos + w_ // 2, 0, Sk)
            invt[t, gi] = 1.0 / np.maximum(hi_ - lo_, 1).astype(np.float32)
    d = dict(common)
    d["xt"] = xt
    d["xcp"] = np.ascontiguousarray(x_prompt[2 * core:2 * core + 2])
    d["ropeq"] = ropeq
    d["mask"] = mask
    d["invt"] = invt
    return d


def make_common(x_sample, norm_mix0, w_in0, conv_w, q_a_norm, w_qb, kv_a_norm, w_kvb, q_norm, k_norm, w_o0,
                norm_mix1, w_pool, pool_scale, norm_mlp, w_up, w_down):
    f = lambda a: np.ascontiguousarray(np.asarray(a, np.float32))
    consts = np.zeros((128, NCONST), np.float32)
    consts[:, C_NM0:C_NM0 + 16] = _fm(norm_mix0[0], 16)
    consts[:, C_NM1:C_NM1 + 16] = _fm(norm_mix1[0], 16)
    consts[:, C_NMLP0:C_NMLP0 + 16] = _fm(norm_mlp[0], 16)
    consts[:, C_NMLP1:C_NMLP1 + 16] = _fm(norm_mlp[1], 16)
    consts[:, C_PSC:C_PSC + 16] = _fm(pool_scale[0], 16)
    consts[:, C_GQA:C_GQA + 4] = _fm(q_a_norm[0], 4)
    consts[:, C_GKVA:C_GKVA + 2] = _fm(kv_a_norm[0], 2)
    qn = np.asarray(q_norm[0], np.float32); kn = np.asarray(k_norm[0], np.float32)
    consts[:, C_GQN] = qn[0:128]
    consts[0:64, C_GQR] = qn[128:192]
    consts[:, C_GKN] = kn[0:128]
    consts[0:64, C_GKR] = kn[128:192]
    cw = np.asarray(conv_w[0], np.float32)
    consts[:, C_CW:C_CW + 24] = cw.reshape(3, 8, 128).transpose(2, 1, 0).reshape(128, 24)
    return {
        "xcs": f(x_sample[0]), "w_in0": f(w_in0[0]), "w_qb": f(w_qb[0]), "w_kvb": f(w_kvb[0]), "w_o0": f(w_o0[0]),
        "w_pool": f(w_pool[0]), "w_up": f(w_up), "w_down": f(w_down), "consts": consts,
        "ident": np.eye(128, dtype=np.float32), "ropec": _rope_table(),
    }


def kernel(x_prompt, x_sample, norm_mix0, w_in0, conv_w, q_a_norm, w_qb, kv_a_norm, w_kvb, q_norm, k_norm, w_o0,
           norm_mix1, w_pool, pool_scale, norm_mlp, w_up, w_down):
    x_prompt = np.asarray(x_prompt, np.float32)
    x_sample = np.asarray(x_sample, np.float32)
    common = make_common(x_sample, norm_mix0, w_in0, conv_w, q_a_norm, w_qb, kv_a_norm, w_kvb, q_norm, k_norm, w_o0,
                         norm_mix1, w_pool, pool_scale, norm_mlp, w_up, w_down)
    nc = build_program(TILES_FULL, CTXS_FULL)
    in_maps = [make_inputs(c, TILES_FULL, CTXS_FULL, x_prompt, x_sample, common) for c in range(8)]
    res = run_bass_kernel_spmd(nc, in_maps, core_ids=list(range(8)))
    y_prompt = np.zeros((16, SP, D), np.float32)
    y_sample = np.zeros((1, SS, D), np.float32)
    for c in range(8):
        y = np.asarray(res.results[c]["y"], np.float32)
        y_prompt[2 * c] = y[0:4].reshape(SP, D)
        y_prompt[2 * c + 1] = y[4:8].reshape(SP, D)
        y_sample[0, c * 1024:(c + 1) * 1024] = y[8:10].reshape(1024, D)
    return (y_prompt, y_sample)
```

```python
import numpy as np
from contextlib import ExitStack
import concourse.bass as bass
import concourse.mybir as mybir
from concourse.bass_utils import run_bass_kernel_spmd

F32 = mybir.dt.float32
BF16 = mybir.dt.bfloat16
AF = mybir.ActivationFunctionType
ALU = mybir.AluOpType

D = 2048
NCH = 16
HALO = 10
TOWN = 512
W = TOWN + 2 * HALO
SP = 2048
SS = 8192
EPS = 1e-6
NSLOT = 3
SAME_ENG_SYNC = False
INTERLEAVE = False
SC_RING = [2, 3, 0]
POOL_OFFLOAD = False
ILV_SCHED = {1: 2, 4: 1, 6: 1, 8: 1, 10: 1, 11: 1, 13: 1}
SLOT_ELEMS = 4096
C_NM0, C_NM1, C_NMLP0, C_NMLP1, C_PSC, C_GQA, C_GKVA, C_GQN, C_GQR, C_GKN, C_GKR, C_CW = 0, 16, 32, 48, 64, 80, 84, 86, 87, 88, 89, 90
NCONST = 114


class Op:
    __slots__ = ("eng", "fn", "waits", "sig", "val", "dma")


class Sched:
    ENGS = ("pe", "act", "dve", "sync", "gq")

    def __init__(self):
        self.q = {e: [] for e in self.ENGS}
        self.lw = {}
        self.rd = {}
        self.dcount = {}
        self.nid = 0

    def op(self, eng, fn, r=(), w=(), dma=None):
        o = Op()
        o.eng = eng; o.fn = fn; o.sig = False; o.val = None; o.dma = dma
        if dma is not None:
            self.dcount[dma] = self.dcount.get(dma, 0) + 16
            o.val = self.dcount[dma]
        deps = []
        for k in r:
            p = self.lw.get(k)
            if p is not None:
                deps.append(p)
            if k[0] == "ps":
                d = self.rd.get(k)
                if d:
                    deps.extend(d.values())
        for k in w:
            p = self.lw.get(k)
            if p is not None:
                deps.append(p)
            d = self.rd.get(k)
            if d:
                deps.extend(d.values())
        waits = []
        seen = set()
        for p in deps:
            if id(p) in seen:
                continue
            seen.add(id(p))
            if p.dma is None and dma is None and p.eng == eng and (eng == "pe" or not SAME_ENG_SYNC):
                continue
            if p.dma is None:
                p.sig = True
            waits.append(p)
        o.waits = waits
        self.nid += 1
        rk = eng if dma is None else ("dma", self.nid)
        for k in r:
            self.rd.setdefault(k, {})[rk] = o
        for k in w:
            self.lw[k] = o
            self.rd[k] = {}
        self.q[eng].append(o)
        return o

    def finalize(self):
        for e in self.ENGS:
            n = 0
            for o in self.q[e]:
                if o.dma is None and o.sig:
                    n += 1
                    o.val = n

    def emit(self, eng, e, esem, dsem, final_waits=()):
        waited = {}

        def wait(p):
            if p.dma is None:
                sem, key = esem[p.eng], p.eng
            else:
                sem, key = dsem[p.dma], p.dma
            if waited.get(key, 0) < p.val:
                e.wait_ge(sem, p.val)
                waited[key] = p.val
        for o in self.q[eng]:
            for p in o.waits:
                wait(p)
            ins = o.fn(e)
            if o.dma is not None:
                ins.then_inc(dsem[o.dma], 16)
            elif o.sig:
                ins.then_inc(esem[eng], 1)
        for p in final_waits:
            wait(p)


def build_program(tiles, ctxs, stage=9):
    NT = len(tiles)
    nc = bass.Bass("TRN2", target_bir_lowering=False)

    def din(name, shape):
        return nc.dram_tensor(name, shape, F32, kind="ExternalInput").ap()
    xt_d = din("xt", [NT, W, D])
    xcp_d = din("xcp", [2, SP, D])
    xcs_d = din("xcs", [SS, D])
    w_in0 = din("w_in0", [D, 3904])
    w_qb = din("w_qb", [512, 1536])
    w_kvb = din("w_kvb", [256, 2048])
    w_o0 = din("w_o0", [D, D])
    w_pool = din("w_pool", [4, 512, 512])
    w_up = din("w_up", [2, D, 8192])
    w_down = din("w_down", [2, 8192, D])
    const_d = din("consts", [128, NCONST])
    ident_d = din("ident", [128, 128])
    ropec_d = din("ropec", [64, 2, SS])
    ropeq_d = din("ropeq", [NT, 64, 2, W])
    mask_d = din("mask", [NT, W])
    invt_d = din("invt", [NT, 4, W])
    y_d = nc.dram_tensor("y", [NT, TOWN, D], F32, kind="ExternalOutput").ap()

    S = Sched()
    with ExitStack() as es:
        def sb(name, shape, dt):
            return es.enter_context(nc.sbuf_tensor(name, shape, dt))
        xT = sb("xT", [128, NCH, W], F32)
        hT = sb("hT", [128, NCH * W], BF16)
        mh = sb("mh", [128, NCH, W], BF16)
        cqg = sb("cqg", [128, 4, W], BF16)
        ckv = sb("ckv", [128, 2, SS], BF16)
        KR = sb("KR", [128, SS], BF16)
        sskr = sb("sskr", [128, SS // 128], F32)
        rk = sb("rk", [128, 32], F32)
        sqK2 = sb("sqK2", [128, 512], BF16)
        stin = sb("stin", [128, D], F32)
        stout = sb("stout", [128, D], F32)
        ubuf = sb("ubuf", [128, W + 2], F32)
        T = [sb(f"T{i}", [128, W], F32) for i in range(6)]
        sq = [sb(f"sq{i}", [128, W], BF16) for i in range(2)]
        ropet = sb("ropet", [64, 2, W], F32)
        wqb = sb("wqb", [128, 4, 1600], BF16)
        ones_lo = sb("ones_lo", [128, 128], BF16)
        wkvb = sb("wkvb", [128, 2, 2048], BF16)
        slots = [sb(f"ws{i}", [128, SLOT_ELEMS], BF16) for i in range(NSLOT)]
        cst = sb("cst", [128, NCONST], F32)
        ident = sb("ident_sb", [128, 128], F32)
        ones = sb("ones", [128, 128], BF16)
        onesf = sb("onesf", [128, 128], F32)
        ps = [es.enter_context(nc.psum_tensor(f"ps{i}", [128, 2, 512], F32)) for i in range(4)]
        esem = {e: es.enter_context(nc.semaphore(f"s_{e}")) for e in ("pe", "act", "dve", "gq")}
        dnames = [f"w{i}" for i in range(NSLOT)] + ["stin", "stout", "cst", "tab", "res0", "res1", "idn", "msk", "inv0", "inv1"]
        dsem = {n: es.enter_context(nc.semaphore(f"d_{n}")) for n in dnames}
        block = es.enter_context(nc.Block())

        def hTc(c, wd=W):
            return hT[:, c * W:c * W + wd]
        Kseg = hT[:, 0:2048]
        Vseg = hT[:, 4 * W:4 * W + 2048].rearrange("p (c n) -> p c n", c=16)
        sqK = hT[:, 8 * W:8 * W + 512]
        Qn = hT[:, 9 * W:10 * W]
        Qr = hT[0:64, 10 * W:11 * W]
        Pt = [hT[:, (11 + i) * W:(12 + i) * W] for i in range(3)]
        K_KSEG = [("hT", c) for c in range(4)]
        K_VSEG = [("hT", c) for c in range(4, 8)]
        K_SQK = [("hT", 8)]
        K_Q = [("hT", 9), ("hT", 10)]
        K_PT = [[("hT", 11 + i)] for i in range(3)]

        def pk(p):
            return [("ps", 0, 0), ("ps", 0, "v")] if p == 0 else [("ps", p)]

        def b2(ap, wd):
            return ap.rearrange("p (b n) -> p b n", b=2)

        def cc(i, n=1):
            return cst[:, i:i + n]

        S.op("sync", lambda e: e.dma_start(out=cst[:], in_=const_d), w=[("cst",)], dma="cst")
        S.op("sync", lambda e: e.dma_start(out=ident[:], in_=ident_d), w=[("ident",)], dma="idn")
        S.op("dve", lambda e: e.memset(wqb[:, :, 1536:1600], 0.0), w=[("wqbpad",)])
        S.op("gq", lambda e: e.dma_start(out=wqb[:, :, 0:1536], in_=w_qb.rearrange("(c p) n -> p c n", p=128)), w=[("wqb",)], dma="res0")
        S.op("gq", lambda e: e.dma_start(out=wkvb[:], in_=w_kvb.rearrange("(c p) n -> p c n", p=128)), w=[("wkvb",)], dma="res1")

        def init_dve(e):
            e.memset(ones[:], 1.0)
            e.memset(ones_lo[:], 1.0)
            e.memset(ones_lo[64:128, :], 0.0)
            e.memset(onesf[:], 1.0)
            e.memset(hT[:], 0.0)
            for j in range(SS // 512):
                e.memset(KR[64:128, j * 512:(j + 1) * 512], 0.0)
            return e.memset(ubuf[:], 0.0)
        S.op("dve", init_dve, w=[("ones",), ("ubuf",)] + [("hT", c) for c in range(NCH)] + [("KR", kb) for kb in range(SS // 512)])

        wstate = {"n": 0}

        def wload(src_ap, kc, ncols):
            i = wstate["n"] % NSLOT
            wstate["n"] += 1
            view = slots[i][:, 0:kc * ncols].rearrange("p (c n) -> p c n", c=kc)
            S.op("gq", lambda e: e.dma_start(out=view, in_=src_ap), w=[("ws", i)], dma=f"w{i}")
            return view, ("ws", i)

        pstate = {"n": 0}

        def next_pair():
            p = pstate["n"] % 4
            pstate["n"] += 1
            return p

        evs = {"n": 0}

        def ev_eng():
            evs["n"] += 1
            return "act" if evs["n"] % 2 else "dve"

        def copy_op(eng, out, in_, r, w):
            if eng == "act":
                S.op("act", lambda e: e.activation(out=out, in_=in_, func=AF.Copy), r=r, w=w)
            else:
                S.op("dve", lambda e: e.tensor_copy(out=out, in_=in_), r=r, w=w)

        def mm(pair, lhs_list, rhs_list, wd, r, M=128, keys=None):
            bw = wd // 2
            nk = len(lhs_list)

            def fn(e):
                ins = None
                for k in range(nk):
                    for b in range(2):
                        ins = e.matmul(ps[pair][0:M, b, 0:bw], lhsT=lhs_list[k], rhs=rhs_list[k][:, b * bw:(b + 1) * bw],
                                       start=(k == 0), stop=(k == nk - 1))
                return ins
            S.op("pe", fn, r=r, w=keys if keys is not None else pk(pair))

        def psv(pair, wd, M=128):
            return ps[pair][0:M, :, 0:wd // 2]

        def load_x(src, wd):
            r0 = 0
            gi = 0
            while r0 < wd:
                R = min(128, wd - r0)
                if gi % 2 == 0:
                    stg, skeys, sdma = stin, [("stin",)], "stin"
                else:
                    stg, skeys, sdma = stout, [("stout", 0), ("stout", 1)], "stout"
                S.op("sync", lambda e, r0=r0, R=R, stg=stg: e.dma_start(out=stg[0:R, :], in_=src[r0:r0 + R, :]), w=skeys, dma=sdma)
                for half in range(2):
                    p = next_pair()
                    pv = ps[p][:].rearrange("p b (j n) -> p (b j) n", n=128)

                    def fn(e, R=R, half=half, pv=pv, stg=stg):
                        ins = None
                        for j in range(8):
                            c = half * 8 + j
                            ins = e.transpose(out=pv[:, j, 0:R], in_=stg[0:R, c * 128:(c + 1) * 128], identity=ident[0:R, 0:R])
                        return ins
                    S.op("pe", fn, r=skeys + [("ident",)], w=pk(p))
                    copy_op(ev_eng(), xT[:, half * 8:half * 8 + 8, r0:r0 + R], pv[:, 0:8, 0:R], r=pk(p),
                            w=[("xT", c) for c in range(half * 8, half * 8 + 8)])
                r0 += R
                gi += 1

        def rstd_from_ms(pair, wd, scale, dst, extra_r=()):
            def fn(e):
                e.activation(out=b2(dst[:, 0:wd], wd), in_=psv(pair, wd), func=AF.Ln, bias=EPS, scale=scale)
                return e.activation(out=dst[:, 0:wd], in_=dst[:, 0:wd], func=AF.Exp, scale=-0.5)
            S.op("act", fn, r=pk(pair) + list(extra_r), w=[("T", id(dst))])

        def tk(t):
            return ("T", id(t))

        def norm_stats(wd, dst):
            p = next_pair()
            bw = wd // 2
            for c in range(NCH):
                s_ = sq[c % 2]
                if c % 2 == 0:
                    S.op("act", lambda e, c=c, s_=s_: e.activation(out=s_[:, 0:wd], in_=xT[:, c, 0:wd], func=AF.Square),
                         r=[("xT", c)], w=[("sq", c % 2)])
                else:
                    S.op("dve", lambda e, c=c, s_=s_: e.tensor_tensor(out=s_[:, 0:wd], in0=xT[:, c, 0:wd], in1=xT[:, c, 0:wd], op=ALU.mult),
                         r=[("xT", c)], w=[("sq", c % 2)])

                def fn(e, c=c, s_=s_):
                    ins = None
                    for b in range(2):
                        ins = e.matmul(ps[p][:, b, 0:bw], lhsT=ones[:], rhs=s_[:, b * bw:(b + 1) * bw], start=(c == 0), stop=(c == NCH - 1))
                    return ins
                S.op("pe", fn, r=[("sq", c % 2), ("ones",)], w=pk(p))
            rstd_from_ms(p, wd, 1.0 / D, dst)

        def norm_to_hT(gcol, wd):
            rstd = T[5]
            norm_stats(wd, rstd)
            for c in range(NCH):
                S.op("dve", lambda e, c=c: e.scalar_tensor_tensor(out=hTc(c, wd), in0=xT[:, c, 0:wd], scalar=cc(gcol + c), in1=rstd[:, 0:wd],
                                                                  op0=ALU.mult, op1=ALU.mult),
                     r=[("xT", c), tk(rstd), ("cst",)], w=[("hT", c)])

        def rope(a, r1, r2, out, wd, rkeys, wkeys):
            cos = ropet[:, 0, 0:wd]
            sin = ropet[:, 1, 0:wd]

            def fn(e):
                e.tensor_tensor(out=r1[0:64, 0:wd], in0=a[0:64, 0:wd], in1=cos[0:64, :], op=ALU.mult)
                e.tensor_tensor(out=r2[0:32, 0:wd], in0=a[32:64, 0:wd], in1=sin[32:64, :], op=ALU.mult)
                e.tensor_tensor(out=r2[32:64, 0:wd], in0=a[0:32, 0:wd], in1=sin[0:32, :], op=ALU.mult)
                e.tensor_tensor(out=out[0:32, 0:wd], in0=r1[0:32, 0:wd], in1=r2[0:32, 0:wd], op=ALU.subtract)
                return e.tensor_tensor(out=out[32:64, 0:wd], in0=r1[32:64, 0:wd], in1=r2[32:64, 0:wd], op=ALU.add)
            S.op("dve", fn, r=[tk(a), ("ropet",)] + list(rkeys), w=[tk(r1), tk(r2)] + list(wkeys))

        def ctx_prep(src, Sk):
            nblk = Sk // 512
            for kb in range(nblk):
                wd = 512
                load_x(src[kb * 512:(kb + 1) * 512, :], wd)
                norm_to_hT(C_NM0, wd)
                S.op("sync", lambda e, kb=kb: e.dma_start(out=ropet[:, :, 0:512], in_=ropec_d[:, :, kb * 512:(kb + 1) * 512]),
                     w=[("ropet",)], dma="tab")
                wa, ka = wload(w_in0[:, 3584:3840].rearrange("(c p) n -> p c n", p=128), 16, 256)
                wb, kb_ = wload(w_in0[:, 3840:3904].rearrange("(c p) n -> p c n", p=128), 16, 64)
                hk = [("hT", c) for c in range(NCH)]
                rh = [hTc(c, wd) for c in range(NCH)]
                pm = [2, 3]
                for m in range(2):
                    mm(pm[m], [wa[:, k, m * 128:(m + 1) * 128] for k in range(NCH)], rh, wd, r=hk + [ka])
                    S.op("act", lambda e, m=m: e.activation(out=b2(sq[m][:, 0:wd], wd), in_=psv(pm[m], wd), func=AF.Square),
                         r=pk(pm[m]), w=[("sq", m)])

                def fn(e):
                    ins = None
                    for m in range(2):
                        for b in range(2):
                            ins = e.matmul(ps[1][:, b, 0:256], lhsT=ones[:], rhs=sq[m][:, b * 256:(b + 1) * 256], start=(m == 0), stop=(m == 1))
                    return ins
                S.op("pe", fn, r=[("sq", 0), ("sq", 1), ("ones",)], w=pk(1))
                rstd_from_ms(1, wd, 1.0 / 256, T[5])
                for m in range(2):
                    S.op("dve", lambda e, m=m, kb=kb: e.scalar_tensor_tensor(out=b2(ckv[:, m, kb * 512:(kb + 1) * 512], wd), in0=psv(pm[m], wd),
                                                                             scalar=cc(C_GKVA + m), in1=b2(T[5][:, 0:wd], wd), op0=ALU.mult, op1=ALU.mult),
                         r=pk(pm[m]) + [tk(T[5]), ("cst",)], w=[("ckv", kb)])
                mm(0, [wb[:, k, 0:64] for k in range(NCH)], rh, wd, r=hk + [kb_], M=64)
                S.op("act", lambda e: e.activation(out=b2(sq[0][0:64, 0:wd], wd), in_=psv(0, wd, 64), func=AF.Square), r=pk(0), w=[("sq", 0)])
                S.op("dve", lambda e: e.tensor_scalar(out=b2(T[4][0:64, 0:wd], wd), in0=psv(0, wd, 64), scalar1=cst[0:64, C_GKR:C_GKR + 1], scalar2=None, op0=ALU.mult),
                     r=pk(0) + [("cst",)], w=[tk(T[4])])

                def fn2(e):
                    ins = None
                    for j in range(4):
                        ins = e.matmul(ps[1][:, 1, 256 + j:257 + j], lhsT=sq[0][0:64, j * 128:(j + 1) * 128], rhs=ones[0:64, 0:1], start=True, stop=True)
                    return ins
                S.op("pe", fn2, r=[("sq", 0), ("ones",)], w=pk(1))
                S.op("dve", lambda e, kb=kb: e.tensor_copy(out=sskr[:, kb * 4:(kb + 1) * 4], in_=ps[1][:, 1, 256:260]), r=pk(1), w=[("sskr", kb)])
                rope(T[4], T[3], T[2], KR[0:64, kb * 512:(kb + 1) * 512], wd, [], [("KR", kb)])

        def in_proj_conv():
            hk = [("hT", c) for c in range(NCH)]
            rh = [hTc(c) for c in range(NCH)]
            for j in range(8):
                views = []
                for base in (1024, 2048, 0):
                    c0 = base + j * 128
                    views.append(wload(w_in0[:, c0:c0 + 128].rearrange("(c p) n -> p c n", p=128), 16, 128))
                (wxc, kxc), (wxi, kxi), (wxb, kxb) = views
                p1, p2, p3 = next_pair(), next_pair(), next_pair()
                mm(p1, [wxc[:, k, :] for k in range(NCH)], rh, W, r=hk + [kxc])
                S.op("act", lambda e, p1=p1: e.activation(out=b2(T[1][:], W), in_=psv(p1, W), func=AF.Copy), r=pk(p1), w=[tk(T[1])])
                mm(p2, [wxi[:, k, :] for k in range(NCH)], rh, W, r=hk + [kxi])
                S.op("act", lambda e, p2=p2: e.activation(out=b2(ubuf[:, 1:W + 1], W), in_=psv(p2, W), func=AF.Copy), r=pk(p2), w=[("ubuf",)])
                mm(p3, [wxb[:, k, :] for k in range(NCH)], rh, W, r=hk + [kxb])

                def fn(e, j=j, p3=p3):
                    cw = C_CW + 3 * j
                    e.tensor_tensor(out=ubuf[:, 1:W + 1], in0=T[1][:], in1=ubuf[:, 1:W + 1], op=ALU.mult)
                    e.tensor_scalar(out=T[3][:], in0=ubuf[:, 1:W + 1], scalar1=cc(cw + 1), scalar2=None, op0=ALU.mult)
                    e.scalar_tensor_tensor(out=T[3][:], in0=ubuf[:, 0:W], scalar=cc(cw), in1=T[3][:], op0=ALU.mult, op1=ALU.add)
                    e.scalar_tensor_tensor(out=T[3][:], in0=ubuf[:, 2:W + 2], scalar=cc(cw + 2), in1=T[3][:], op0=ALU.mult, op1=ALU.add)
                    return e.tensor_tensor(out=b2(mh[:, j, :], W), in0=psv(p3, W), in1=b2(T[3][:], W), op=ALU.mult)
                S.op("dve", fn, r=[tk(T[1]), ("ubuf",), ("cst",)] + pk(p3), w=[("ubuf",), tk(T[3]), ("mh", j)])
            pms = next_pair()
            for half in range(2):
                wq, kq = wload(w_in0[:, 3072 + half * 256:3072 + (half + 1) * 256].rearrange("(c p) n -> p c n", p=128), 16, 256)
                for mi in range(2):
                    m = half * 2 + mi
                    p = next_pair()
                    if p == pms:
                        p = next_pair()
                    mm(p, [wq[:, k, mi * 128:(mi + 1) * 128] for k in range(NCH)], rh, W, r=hk + [kq])
                    S.op("act", lambda e, p=p, m=m: e.activation(out=b2(sq[m % 2][:], W), in_=psv(p, W), func=AF.Square), r=pk(p), w=[("sq", m % 2)])
                    S.op("dve", lambda e, p=p, m=m: e.tensor_scalar(out=b2(cqg[:, m, :], W), in0=psv(p, W), scalar1=cc(C_GQA + m), scalar2=None, op0=ALU.mult),
                         r=pk(p) + [("cst",)], w=[("cqg", m)])

                    def fn(e, m=m):
                        ins = None
                        for b in range(2):
                            ins = e.matmul(ps[pms][:, b, 0:W // 2], lhsT=ones[:], rhs=sq[m % 2][:, b * (W // 2):(b + 1) * (W // 2)], start=(m == 0), stop=(m == 3))
                        return ins
                    S.op("pe", fn, r=[("sq", m % 2), ("ones",)], w=pk(pms))
            rstd_from_ms(pms, W, 1.0 / 512, T[2])

        def attention(Sk):
            bw = W // 2
            nseg = Sk // 2048
            scq = T[2]
            acc, rden, f, a_, r2 = T[0], T[1], T[3], T[4], T[5]
            stb = stout[:].bitcast(BF16)
            KB = [Kseg, stb[:, 0:2048]]
            VB = [Vseg, stb[:, 2048:4096].rearrange("p (c n) -> p c n", c=16)]
            kKB = [K_KSEG, [("stout", 0)]]
            kVB = [K_VSEG, [("stout", 1)]]
            SQB = [sqK, sqK2[:, :]]
            kSQB = [K_SQK, [("sqK2",)]]
            RKB = [rk[:, 0:16], rk[:, 16:32]]
            kRKB = [[("rk", 0)], [("rk", 1)]]
            QNB = [Qn, hT[:, 14 * W:15 * W]]
            QRB = [Qr, hT[0:64, 15 * W:16 * W]]
            QRF = [hT[:, 10 * W:11 * W], hT[:, 15 * W:16 * W]]

            def zq(e):
                e.memset(hT[64:128, 10 * W:11 * W], 0.0)
                return e.memset(hT[64:128, 15 * W:16 * W], 0.0)
            S.op("dve", zq, w=[("hT", 10), ("hT", 15)])
            kQB = [K_Q, [("hT", 14), ("hT", 15)]]
            units = [(h, sg) for h in range(8) for sg in range(nseg)]
            P0 = [("ps", 0, 0), ("ps", 0, "v")]

            def prep(u):
                h, sg = units[u]
                par = u % 2
                k0 = sg * 2048
                Kb, Vb, sqb, rkb = KB[par], VB[par], SQB[par], RKB[par]
                kK, kV, kS, kR = kKB[par], kVB[par], kSQB[par], kRKB[par]
                BK = [("ps", 0, 0), ("ps", 0, "v")]
                S.op("dve", lambda e: e.memset(rkb, 0.0), w=kR)
                for kb in range(4):
                    gkb = (k0 + kb * 512) // 512
                    bank = kb % 2

                    def fnk(e, kb=kb, bank=bank):
                        ins = None
                        for r in range(2):
                            ins = e.matmul(ps[0][:, bank, 0:512], lhsT=wkvb[:, r, h * 256:h * 256 + 128], rhs=ckv[:, r, k0 + kb * 512:k0 + (kb + 1) * 512],
                                           start=(r == 0), stop=(r == 1))
                        return ins
                    S.op("pe", fnk, r=[("wkvb",), ("ckv", gkb)], w=[BK[bank]])
                    S.op("dve", lambda e, kb=kb, bank=bank: e.tensor_scalar(out=Kb[:, kb * 512:(kb + 1) * 512], in0=ps[0][:, bank, 0:512], scalar1=cc(C_GKN), scalar2=None, op0=ALU.mult),
                         r=[BK[bank], ("cst",)], w=kK)
                yield
                for kc2 in range(8):
                    gkb = (k0 + kc2 * 256) // 512
                    bank = kc2 % 2

                    def fnv(e, kc2=kc2, bank=bank):
                        ins = None
                        for j in range(2):
                            ks = k0 + (kc2 * 2 + j) * 128
                            for r in range(2):
                                ins = e.matmul(ps[0][:, bank, j * 256:(j + 1) * 256], lhsT=ckv[:, r, ks:ks + 128], rhs=wkvb[:, r, h * 256:h * 256 + 256],
                                               start=(r == 0), stop=(r == 1))
                        return ins
                    S.op("pe", fnv, r=[("wkvb",), ("ckv", gkb)], w=[BK[bank]])

                    def fna(e, kc2=kc2, bank=bank):
                        ins = None
                        for j in range(2):
                            kc = kc2 * 2 + j
                            ins = e.activation(out=sqb[:, j * 128:(j + 1) * 128], in_=ps[0][:, bank, j * 256:j * 256 + 128], func=AF.Square,
                                               accum_out=rkb[:, kc:kc + 1])
                        return ins
                    S.op("act", fna, r=[BK[bank]] + kR, w=kS + kR)
                    S.op("dve", lambda e, kc2=kc2, bank=bank: e.tensor_copy(out=Vb[:, kc2 * 2:kc2 * 2 + 2, :],
                                                                            in_=ps[0][:, bank, :].rearrange("p (c n) -> p c n", c=2)[:, :, 128:256]),
                         r=[BK[bank]], w=kV)
                yield
                S.op("dve", lambda e: e.tensor_tensor(out=rkb, in0=rkb, in1=sskr[:, sg * 16:(sg + 1) * 16], op=ALU.add),
                     r=kR + [("sskr", sg * 4 + i) for i in range(4)], w=kR)
                S.op("act", lambda e: e.activation(out=rkb, in_=rkb, func=AF.Ln, bias=192.0 * EPS, scale=1.0), r=kR, w=kR)
                S.op("dve", lambda e: e.tensor_scalar(out=rkb, in0=rkb, scalar1=-0.5, scalar2=None, op0=ALU.mult), r=kR, w=kR)
                S.op("act", lambda e: e.activation(out=rkb, in_=rkb, func=AF.Exp), r=kR, w=kR)
                yield
                if sg != 0:
                    return
                qp = h % 2
                Qn_, Qr_, kQ = QNB[qp], QRB[qp], kQB[qp]
                rq = [cqg[:, r, :] for r in range(4)]
                kq = [("cqg", r) for r in range(4)] + [("wqb",)]
                mm(0, [wqb[:, r, h * 192:h * 192 + 128] for r in range(4)], rq, W, r=kq, keys=P0)
                S.op("act", lambda e: e.activation(out=b2(sq[0][:], W), in_=psv(0, W), func=AF.Square), r=P0, w=[("sq", 0)])
                S.op("dve", lambda e: e.tensor_copy(out=b2(ubuf[:, 0:W], W), in_=psv(0, W)), r=P0, w=[("ubuf",)])
                yield
                mm(0, [wqb[:, r, h * 192 + 128:h * 192 + 256] for r in range(4)], rq, W, r=kq + [("wqbpad",)], keys=P0)
                S.op("act", lambda e: e.activation(out=b2(sq[1][:], W), in_=psv(0, W), func=AF.Square), r=P0, w=[("sq", 1)])
                S.op("dve", lambda e: e.tensor_scalar(out=b2(a_[0:64, :], W), in0=psv(0, W, 64), scalar1=cst[0:64, C_GQR:C_GQR + 1], scalar2=None, op0=ALU.mult),
                     r=P0 + [("cst",)], w=[tk(a_)])
                yield

                def fnm(e):
                    ins = None
                    for b in range(2):
                        e.matmul(ps[0][:, b, 0:bw], lhsT=ones[:], rhs=sq[0][:, b * bw:(b + 1) * bw], start=True, stop=False)
                        ins = e.matmul(ps[0][:, b, 0:bw], lhsT=ones_lo[:], rhs=sq[1][:, b * bw:(b + 1) * bw], start=False, stop=True)
                    return ins
                S.op("pe", fnm, r=[("sq", 0), ("sq", 1), ("ones",)], w=P0)

                def fnf(e):
                    e.tensor_tensor(out=b2(f[:], W), in0=psv(0, W), in1=b2(scq[:], W), op=ALU.mult)
                    return e.tensor_tensor(out=f[:], in0=f[:], in1=scq[:], op=ALU.mult)
                S.op("dve", fnf, r=P0 + [tk(scq)], w=[tk(f)])
                def fnl(e):
                    e.activation(out=f[:], in_=f[:], func=AF.Ln, bias=EPS, scale=1.0 / 192)
                    return e.activation(out=f[:], in_=f[:], func=AF.Exp, scale=-0.5)
                S.op("act", fnl, r=[tk(f)], w=[tk(f)])
                yield

                def fnq(e):
                    e.tensor_tensor(out=f[:], in0=f[:], in1=scq[:], op=ALU.mult)
                    e.scalar_tensor_tensor(out=Qn_, in0=ubuf[:, 0:W], scalar=cc(C_GQN), in1=f[:], op0=ALU.mult, op1=ALU.mult)
                    return e.tensor_tensor(out=a_[0:64, :], in0=a_[0:64, :], in1=f[0:64, :], op=ALU.mult)
                S.op("dve", fnq, r=[tk(f), tk(scq), tk(a_), ("ubuf",), ("cst",)], w=[tk(f), tk(a_)] + kQ)
                yield
                rope(a_, f, r2, Qr_, W, [], kQ)
                yield

            def drain(g, n=None):
                if g is None:
                    return
                k = 0
                for _ in g:
                    k += 1
                    if n is not None and k >= n:
                        return

            sc_i = {"n": 0}
            pt_i = {"n": 0}
            drain(prep(0))
            for u, (h, sg) in enumerate(units):
                par = u % 2
                qp = h % 2
                k0 = sg * 2048
                Kb, Vb, rkb = KB[par], VB[par], RKB[par]
                kK, kV, kR = kKB[par], kVB[par], kRKB[par]
                Qn_, Qr_, kQ = QNB[qp], QRF[qp], kQB[qp]
                nxt = prep(u + 1) if u + 1 < len(units) else None
                if not INTERLEAVE:
                    drain(nxt)
                    nxt = None
                prs = {}

                def scores(kc):
                    pr = SC_RING[sc_i["n"] % len(SC_RING)]
                    sc_i["n"] += 1
                    prs[kc] = pr
                    gkb = (k0 + kc * 128) // 512

                    def fn(e, kc=kc, pr=pr, Kb=Kb, Qn_=Qn_, Qr_=Qr_, k0=k0):
                        ins = None
                        for b in range(2):
                            e.matmul(ps[pr][:, b, 0:bw], lhsT=Kb[:, kc * 128:(kc + 1) * 128], rhs=Qn_[:, b * bw:(b + 1) * bw], start=True, stop=False)
                            ins = e.matmul(ps[pr][:, b, 0:bw], lhsT=KR[:, k0 + kc * 128:k0 + (kc + 1) * 128], rhs=Qr_[:, b * bw:(b + 1) * bw], start=False, stop=True)
                        return ins
                    S.op("pe", fn, r=kK + kQ + [("KR", gkb)], w=pk(pr))
                depth = len(SC_RING) - 1
                for k_ in range(depth):
                    scores(k_)
                for kc in range(16):
                    if kc + depth < 16:
                        scores(kc + depth)
                    pr = prs[kc]
                    pi = pt_i["n"] % 3
                    pt_i["n"] += 1
                    ptile = Pt[pi]
                    S.op("act", lambda e, kc=kc, pr=pr, ptile=ptile, rkb=rkb: e.activation(out=b2(ptile, W), in_=psv(pr, W), func=AF.Exp, scale=rkb[:, kc:kc + 1]),
                         r=pk(pr) + kR, w=K_PT[pi])
                    first = (sg == 0 and kc == 0)
                    last = (sg == nseg - 1 and kc == 15)

                    def fnpv(e, kc=kc, ptile=ptile, first=first, last=last, Vb=Vb):
                        ins = None
                        for b in range(2):
                            ins = e.matmul(ps[1][:, b, 0:bw], lhsT=Vb[:, kc, :], rhs=ptile[:, b * bw:(b + 1) * bw], start=first, stop=last)
                        return ins
                    S.op("pe", fnpv, r=kV + K_PT[pi], w=pk(1))
                    if first:
                        S.op("dve", lambda e, ptile=ptile: e.tensor_copy(out=acc[:], in_=ptile), r=K_PT[pi], w=[tk(acc)])
                    else:
                        S.op("dve", lambda e, ptile=ptile: e.tensor_tensor(out=acc[:], in0=acc[:], in1=ptile, op=ALU.add), r=K_PT[pi] + [tk(acc)], w=[tk(acc)])
                    if nxt is not None:
                        for _ in range(ILV_SCHED.get(kc, 0)):
                            next(nxt, None)
                drain(nxt)
                if sg == nseg - 1:
                    pd = SC_RING[sc_i["n"] % len(SC_RING)]
                    sc_i["n"] += 1

                    def fnd(e, pd=pd):
                        ins = None
                        for b in range(2):
                            ins = e.matmul(ps[pd][:, b, 0:bw], lhsT=onesf[:], rhs=acc[:, b * bw:(b + 1) * bw], start=True, stop=True)
                        return ins
                    S.op("pe", fnd, r=[tk(acc), ("ones",)], w=pk(pd))
                    def fnr_(e, pd=pd):
                        e.activation(out=b2(rden[:], W), in_=psv(pd, W), func=AF.Ln)
                        return e.activation(out=rden[:], in_=rden[:], func=AF.Exp, scale=-1.0)
                    S.op("act", fnr_, r=pk(pd), w=[tk(rden)])
                    S.op("dve", lambda e, h=h: e.tensor_tensor(out=b2(mh[:, 8 + h, :], W), in0=psv(1, W), in1=b2(rden[:], W), op=ALU.mult),
                         r=pk(1) + [tk(rden)], w=[("mh", 8 + h)])

        def resid_add(p, c):
            S.op("dve", lambda e: e.tensor_tensor(out=b2(xT[:, c, :], W), in0=psv(p, W), in1=b2(xT[:, c, :], W), op=ALU.add),
                 r=pk(p) + [("xT", c)], w=[("xT", c)])

        def out_proj():
            mk = [("mh", c) for c in range(NCH)]
            rm = [mh[:, c, :] for c in range(NCH)]
            for ch in range(8):
                wv, kk = wload(w_o0[:, ch * 256:(ch + 1) * 256].rearrange("(c p) n -> p c n", p=128), 16, 256)
                for mi in range(2):
                    p = next_pair()
                    mm(p, [wv[:, k, mi * 128:(mi + 1) * 128] for k in range(NCH)], rm, W, r=mk + [kk])
                    resid_add(p, ch * 2 + mi)

        def mlp(layer):
            norm_to_hT(C_NMLP0 if layer == 0 else C_NMLP1, W)
            hk = [("hT", c) for c in range(NCH)]
            rh = [hTc(c) for c in range(NCH)]
            for g in range(8):
                hs = (g % 2) * 8
                for q4 in range(4):
                    f0 = g * 1024 + q4 * 256
                    wv, kk = wload(w_up[layer, :, f0:f0 + 256].rearrange("(c p) n -> p c n", p=128), 16, 256)
                    for mi in range(2):
                        p = next_pair()
                        fc = q4 * 2 + mi
                        mm(p, [wv[:, k, mi * 128:(mi + 1) * 128] for k in range(NCH)], rh, W, r=hk + [kk])
                        tt = T[fc % 2]
                        S.op("act", lambda e, p=p, tt=tt: e.activation(out=b2(tt[:], W), in_=psv(p, W), func=AF.Relu), r=pk(p), w=[tk(tt)])
                        S.op("dve", lambda e, tt=tt, fc=fc, hs=hs: e.tensor_tensor(out=mh[:, hs + fc, :], in0=tt[:], in1=tt[:], op=ALU.mult),
                             r=[tk(tt)], w=[("mh", hs + fc)])
                mk = [("mh", hs + k) for k in range(8)]
                rm = [mh[:, hs + k, :] for k in range(8)]
                for dq in range(4):
                    wv, kk = wload(w_down[layer, g * 1024:(g + 1) * 1024, dq * 512:(dq + 1) * 512].rearrange("(c p) n -> p c n", p=128), 8, 512)
                    for dc in range(4):
                        p = next_pair()
                        mm(p, [wv[:, k, dc * 128:(dc + 1) * 128] for k in range(8)], rm, W, r=mk + [kk])
                        resid_add(p, dq * 4 + dc)

        def wsum(src, ta, tb, L, eng_fn_list):
            cur = src
            bufs = [ta, tb]
            lo, hi = 0, W
            for s in range(L):
                dst = bufs[s % 2]
                if s == 0:
                    nlo, nhi = lo + 1, hi
                    eng_fn_list.append(lambda e, cur=cur, dst=dst, nlo=nlo, nhi=nhi: e.tensor_tensor(
                        out=dst[:, nlo:nhi], in0=cur[:, nlo - 1:nhi - 1], in1=cur[:, nlo:nhi], op=ALU.add))
                else:
                    sh = 1 << (s - 1)
                    nlo, nhi = lo + sh, hi - sh
                    eng_fn_list.append(lambda e, cur=cur, dst=dst, nlo=nlo, nhi=nhi, sh=sh: e.tensor_tensor(
                        out=dst[:, nlo:nhi], in0=cur[:, nlo - sh:nhi - sh], in1=cur[:, nlo + sh:nhi + sh], op=ALU.add))
                lo, hi = nlo, nhi
                cur = dst
            return cur

        def pool_layer(ti):
            rstd = T[5]
            norm_stats(W, rstd)
            maskt = T[4]
            S.op("sync", lambda e: e.dma_start(out=maskt[:], in_=mask_d[ti:ti + 1, :].partition_broadcast(128)), w=[tk(maskt)], dma="msk")
            S.op("dve", lambda e: e.tensor_tensor(out=rstd[:], in0=rstd[:], in1=maskt[:], op=ALU.mult), r=[tk(rstd), tk(maskt)], w=[tk(rstd)])
            o0, o1 = HALO, HALO + TOWN
            wl = {}
            for gi in range(3):
                wl[gi] = wload(w_pool[gi].rearrange("(c p) n -> p c n", p=128), 4, 512)

            def make_inv(gi, ta, tb, inv, wkeys):
                S.op("sync", lambda e: e.dma_start(out=inv[:, 0:W], in_=invt_d[ti, gi:gi + 1, :].partition_broadcast(128)), w=wkeys, dma=f"inv{gi % 2}")

            def group_elementwise(gi, eng, h32, ta, tb, inv, tkeys):
                L = gi + 1
                for cq_ in range(4):
                    c = gi * 4 + cq_
                    if eng == "dve":
                        fl = [lambda e, c=c: e.scalar_tensor_tensor(out=h32[:, 0:W], in0=xT[:, c, :], scalar=cc(C_NM1 + c), in1=rstd[:], op0=ALU.mult, op1=ALU.mult)]
                    else:
                        fl = [lambda e, c=c: e.tensor_scalar(out=h32[:, 0:W], in0=xT[:, c, :], scalar1=cc(C_NM1 + c), scalar2=None, op0=ALU.mult),
                              lambda e: e.tensor_tensor(out=h32[:, 0:W], in0=h32[:, 0:W], in1=rstd[:], op=ALU.mult)]
                    s_ = wsum(h32, ta, tb, L, fl)
                    other = tb if s_ is ta else ta
                    fl.append(lambda e, s_=s_, other=other: e.tensor_tensor(out=other[:, o0:o1], in0=s_[:, o0:o1], in1=inv[:, o0:o1], op=ALU.mult))
                    fl.append(lambda e, c=c, other=other: e.tensor_tensor(out=hTc(c)[:, o0:o1], in0=other[:, o0:o1], in1=h32[:, o0:o1], op=ALU.subtract))

                    def run2(e, fl=fl):
                        ins = None
                        for f_ in fl:
                            ins = f_(e)
                        return ins
                    S.op(eng, run2, r=[("xT", c), tk(rstd), ("cst",)] + tkeys, w=tkeys + [("hT", c)])

            def group_matmul(gi, wv, kk):
                hk = [("hT", gi * 4 + k) for k in range(4)]
                rh = [hTc(gi * 4 + k) for k in range(4)]
                for m in range(4):
                    p = next_pair()
                    c = gi * 4 + m
                    mm(p, [wv[:, k, m * 128:(m + 1) * 128] for k in range(4)], rh, W, r=hk + [kk])
                    S.op("dve", lambda e, p=p, c=c: e.scalar_tensor_tensor(out=b2(xT[:, c, :], W), in0=psv(p, W), scalar=cc(C_PSC + c), in1=b2(xT[:, c, :], W),
                                                                           op0=ALU.mult, op1=ALU.add),
                         r=pk(p) + [("xT", c), ("cst",)], w=[("xT", c)])
            TK = [tk(T[0]), tk(T[1]), tk(T[2]), tk(T[3])]
            invb = [T[3], ubuf[:, 0:W]]
            invk = [[tk(T[3])], [("ubuf",)]]
            TK3 = [tk(T[0]), tk(T[1]), tk(T[2])]
            for gi in range(2):
                make_inv(gi, None, None, invb[gi % 2], invk[gi % 2])
            for gi in range(4):
                group_elementwise(gi, "dve", T[0], T[1], T[2], invb[gi % 2], TK3 + invk[gi % 2])
                if gi + 2 < 4:
                    make_inv(gi + 2, None, None, invb[gi % 2], invk[gi % 2])
                if gi < 3:
                    group_matmul(gi, *wl[gi])
            wv3, kk3 = wload(w_pool[3].rearrange("(c p) n -> p c n", p=128), 4, 512)
            group_matmul(3, wv3, kk3)

        stores = []

        def store(ti):
            for tg in range(4):
                col0 = HALO + tg * 128
                if tg % 2 == 0:
                    stg, skeys, sdma = stout, [("stout", 0), ("stout", 1)], "stout"
                else:
                    stg, skeys, sdma = stin, [("stin",), ("stin",)], "stin"
                for half in range(2):
                    p = next_pair()
                    pv = ps[p][:].rearrange("p b (j n) -> p (b j) n", n=128)

                    def fn(e, half=half, pv=pv, col0=col0):
                        ins = None
                        for j in range(8):
                            c = half * 8 + j
                            ins = e.transpose(out=pv[:, j, :], in_=xT[:, c, col0:col0 + 128], identity=ident[:, :])
                        return ins
                    S.op("pe", fn, r=[("xT", c) for c in range(half * 8, half * 8 + 8)] + [("ident",)], w=pk(p))
                    copy_op(ev_eng(), stg[:, half * 1024:(half + 1) * 1024].rearrange("p (j n) -> p j n", n=128), pv[:, 0:8, :], r=pk(p), w=[skeys[half]])
                o = S.op("sync", lambda e, tg=tg, stg=stg: e.dma_start(out=y_d[ti, tg * 128:(tg + 1) * 128, :], in_=stg[:, :]), r=skeys, dma=sdma)
                stores.append(o)

        ti = 0
        for ci, (kind, idx) in enumerate(ctxs):
            Sk = SP if kind == "p" else SS
            src = xcp_d[idx] if kind == "p" else xcs_d
            ctx_prep(src, Sk)
            while ti < NT and tiles[ti] == ci:
                load_x(xt_d[ti], W)
                S.op("sync", lambda e, ti=ti: e.dma_start(out=ropet[:, :, :], in_=ropeq_d[ti]), w=[("ropet",)], dma="tab")
                norm_to_hT(C_NM0, W)
                if stage == -2:
                    in_proj_conv()
                    attention(Sk)
                    S.op("dve", lambda e: e.tensor_copy(out=xT[:, 0, 12:28], in_=rk[:, :]), r=[("rk",)], w=[("xT", 0)])
                    S.op("dve", lambda e: e.tensor_copy(out=xT[:, 1, 12:28], in_=sskr[:, 0:16]), r=[("sskr", i) for i in range(4)], w=[("xT", 1)])
                if stage == -1:
                    in_proj_conv()
                    attention(Sk)
                    for c in range(NCH):
                        S.op("dve", lambda e, c=c: e.tensor_copy(out=xT[:, c, :], in_=mh[:, c, :]), r=[("mh", c)], w=[("xT", c)])
                if stage >= 1:
                    in_proj_conv()
                    attention(Sk)
                    out_proj()
                if stage >= 2:
                    mlp(0)
                if stage >= 3:
                    pool_layer(ti)
                if stage >= 4:
                    mlp(1)
                store(ti)
                ti += 1

        S.finalize()
        emap = {"pe": "tensor", "act": "scalar", "dve": "vector", "sync": "sync", "gq": "gpsimd"}

        @block.sync
        def _(e):
            S.emit("sync", e, esem, dsem, final_waits=stores[-2:])

        @block.gpsimd
        def _(e):
            S.emit("gq", e, esem, dsem)

        @block.tensor
        def _(e):
            S.emit("pe", e, esem, dsem)

        @block.scalar
        def _(e):
            S.emit("act", e, esem, dsem)

        @block.vector
        def _(e):
            S.emit("dve", e, esem, dsem)
    return nc


def _fm(g, nch):
    return np.ascontiguousarray(np.asarray(g, np.float32).reshape(nch, 128).T)


def _rope_table():
    inv = (1.0 / (np.float32(10000.0) ** (np.arange(0, 64, 2, dtype=np.float32) / np.float32(64)))).astype(np.float32)
    ang = np.arange(SS, dtype=np.float32)[:, None] * inv[None, :]
    cos = np.cos(ang).astype(np.float32).T
    sin = np.sin(ang).astype(np.float32).T
    tab = np.zeros((64, 2, SS), np.float32)
    tab[0:32, 0] = cos; tab[32:64, 0] = cos
    tab[0:32, 1] = sin; tab[32:64, 1] = sin
    return tab


TILES_FULL = [0, 0, 0, 0, 1, 1, 1, 1, 2, 2]
CTXS_FULL = [("p", 0), ("p", 1), ("s", 0)]


def make_inputs(core, tiles, ctxs, x_prompt, x_sample, common):
    NT = len(tiles)
    xt = np.zeros((NT, W, D), np.float32)
    ropeq = np.zeros((NT, 64, 2, W), np.float32)
    mask = np.zeros((NT, W), np.float32)
    invt = np.ones((NT, 4, W), np.float32)
    tab = common["ropec"]
    per_ctx = {}
    for t, ci in enumerate(tiles):
        k = per_ctx.get(ci, 0)
        per_ctx[ci] = k + 1
        kind, idx = ctxs[ci]
        if kind == "p":
            seq = x_prompt[2 * core + idx]
            Sk = SP
            start = k * TOWN
        else:
            seq = x_sample[0]
            Sk = SS
            start = core * 1024 + k * TOWN
        pos = np.arange(start - HALO, start + TOWN + HALO)
        valid = (pos >= 0) & (pos < Sk)
        pc = np.clip(pos, 0, Sk - 1)
        xt[t][valid] = seq[pos[valid]]
        ropeq[t] = tab[:, :, pc]
        mask[t] = valid.astype(np.float32)
        for gi, w_ in enumerate((2, 4, 8, 16)):
            lo_ = np.clip(pos - w_ // 2, 0, Sk)
            hi_ = np.clip(pos + w_ // 2, 0, Sk)
            invt[t, gi] = 1.0 / np.maximum(hi_ - lo_, 1).astype(np.float32)
    d = dict(common)
    d["xt"] = xt
    d["xcp"] = np.ascontiguousarray(x_prompt[2 * core:2 * core + 2])
    d["ropeq"] = ropeq
    d["mask"] = mask
    d["invt"] = invt
    return d


def make_common(x_sample, norm_mix0, w_in0, conv_w, q_a_norm, w_qb, kv_a_norm, w_kvb, q_norm, k_norm, w_o0,
                norm_mix1, w_pool, pool_scale, norm_mlp, w_up, w_down):
    f = lambda a: np.ascontiguousarray(np.asarray(a, np.float32))
    consts = np.zeros((128, NCONST), np.float32)
    consts[:, C_NM0:C_NM0 + 16] = _fm(norm_mix0[0], 16)
    consts[:, C_NM1:C_NM1 + 16] = _fm(norm_mix1[0], 16)
    consts[:, C_NMLP0:C_NMLP0 + 16] = _fm(norm_mlp[0], 16)
    consts[:, C_NMLP1:C_NMLP1 + 16] = _fm(norm_mlp[1], 16)
    consts[:, C_PSC:C_PSC + 16] = _fm(pool_scale[0], 16)
    consts[:, C_GQA:C_GQA + 4] = _fm(q_a_norm[0], 4)
    consts[:, C_GKVA:C_GKVA + 2] = _fm(kv_a_norm[0], 2)
    qn = np.asarray(q_norm[0], np.float32); kn = np.asarray(k_norm[0], np.float32)
    consts[:, C_GQN] = qn[0:128]
    consts[0:64, C_GQR] = qn[128:192]
    consts[:, C_GKN] = kn[0:128]
    consts[0:64, C_GKR] = kn[128:192]
    cw = np.asarray(conv_w[0], np.float32)
    consts[:, C_CW:C_CW + 24] = cw.reshape(3, 8, 128).transpose(2, 1, 0).reshape(128, 24)
    return {
        "xcs": f(x_sample[0]), "w_in0": f(w_in0[0]), "w_qb": f(w_qb[0]), "w_kvb": f(w_kvb[0]), "w_o0": f(w_o0[0]),
        "w_pool": f(w_pool[0]), "w_up": f(w_up), "w_down": f(w_down), "consts": consts,
        "ident": np.eye(128, dtype=np.float32), "ropec": _rope_table(),
    }


def kernel(x_prompt, x_sample, norm_mix0, w_in0, conv_w, q_a_norm, w_qb, kv_a_norm, w_kvb, q_norm, k_norm, w_o0,
           norm_mix1, w_pool, pool_scale, norm_mlp, w_up, w_down):
    x_prompt = np.asarray(x_prompt, np.float32)
    x_sample = np.asarray(x_sample, np.float32)
    common = make_common(x_sample, norm_mix0, w_in0, conv_w, q_a_norm, w_qb, kv_a_norm, w_kvb, q_norm, k_norm, w_o0,
                         norm_mix1, w_pool, pool_scale, norm_mlp, w_up, w_down)
    nc = build_program(TILES_FULL, CTXS_FULL)
    in_maps = [make_inputs(c, TILES_FULL, CTXS_FULL, x_prompt, x_sample, common) for c in range(8)]
    res = run_bass_kernel_spmd(nc, in_maps, core_ids=list(range(8)))
    y_prompt = np.zeros((16, SP, D), np.float32)
    y_sample = np.zeros((1, SS, D), np.float32)
    for c in range(8):
        y = np.asarray(res.results[c]["y"], np.float32)
        y_prompt[2 * c] = y[0:4].reshape(SP, D)
        y_prompt[2 * c + 1] = y[4:8].reshape(SP, D)
        y_sample[0, c * 1024:(c + 1) * 1024] = y[8:10].reshape(1024, D)
    return (y_prompt, y_sample)
```

```python
import numpy as np
from contextlib import ExitStack
import concourse.bass as bass
import concourse.mybir as mybir
from concourse.bass_utils import run_bass_kernel_spmd

F32 = mybir.dt.float32
BF16 = mybir.dt.bfloat16
AF = mybir.ActivationFunctionType
ALU = mybir.AluOpType

D = 2048
NCH = 16
HALO = 12
TOWN = 512
W = TOWN + 2 * HALO
SP = 2048
SS = 8192
EPS = 1e-6
NSLOT = 3
SAME_ENG_SYNC = False
INTERLEAVE = False
SC_RING = [2, 3, 0]
POOL_OFFLOAD = False
ILV_SCHED = {1: 2, 4: 1, 6: 1, 8: 1, 10: 1, 11: 1, 13: 1}
SLOT_ELEMS = 4096
C_NM0, C_NM1, C_NMLP0, C_NMLP1, C_PSC, C_GQA, C_GKVA, C_GQN, C_GQR, C_GKN, C_GKR, C_CW = 0, 16, 32, 48, 64, 80, 84, 86, 87, 88, 89, 90
NCONST = 114


class Op:
    __slots__ = ("eng", "fn", "waits", "sig", "val", "dma")


class Sched:
    ENGS = ("pe", "act", "dve", "sync", "gq")

    def __init__(self):
        self.q = {e: [] for e in self.ENGS}
        self.lw = {}
        self.rd = {}
        self.dcount = {}
        self.nid = 0

    def op(self, eng, fn, r=(), w=(), dma=None):
        o = Op()
        o.eng = eng; o.fn = fn; o.sig = False; o.val = None; o.dma = dma
        if dma is not None:
            self.dcount[dma] = self.dcount.get(dma, 0) + 16
            o.val = self.dcount[dma]
        deps = []
        for k in r:
            p = self.lw.get(k)
            if p is not None:
                deps.append(p)
            if k[0] == "ps":
                d = self.rd.get(k)
                if d:
                    deps.extend(d.values())
        for k in w:
            p = self.lw.get(k)
            if p is not None:
                deps.append(p)
            d = self.rd.get(k)
            if d:
                deps.extend(d.values())
        waits = []
        seen = set()
        for p in deps:
            if id(p) in seen:
                continue
            seen.add(id(p))
            if p.dma is None and dma is None and p.eng == eng and (eng == "pe" or not SAME_ENG_SYNC):
                continue
            if p.dma is None:
                p.sig = True
            waits.append(p)
        o.waits = waits
        self.nid += 1
        rk = eng if dma is None else ("dma", self.nid)
        for k in r:
            self.rd.setdefault(k, {})[rk] = o
        for k in w:
            self.lw[k] = o
            self.rd[k] = {}
        self.q[eng].append(o)
        return o

    def finalize(self):
        for e in self.ENGS:
            n = 0
            for o in self.q[e]:
                if o.dma is None and o.sig:
                    n += 1
                    o.val = n

    def emit(self, eng, e, esem, dsem, final_waits=()):
        waited = {}

        def wait(p):
            if p.dma is None:
                sem, key = esem[p.eng], p.eng
            else:
                sem, key = dsem[p.dma], p.dma
            if waited.get(key, 0) < p.val:
                e.wait_ge(sem, p.val)
                waited[key] = p.val
        for o in self.q[eng]:
            for p in o.waits:
                wait(p)
            ins = o.fn(e)
            if o.dma is not None:
                ins.then_inc(dsem[o.dma], 16)
            elif o.sig:
                ins.then_inc(esem[eng], 1)
        for p in final_waits:
            wait(p)


def build_program(tiles, ctxs, stage=9):
    NT = len(tiles)
    nc = bass.Bass("TRN2", target_bir_lowering=False)

    def din(name, shape):
        return nc.dram_tensor(name, shape, F32, kind="ExternalInput").ap()
    xt_d = din("xt", [NT, W, D])
    xcp_d = din("xcp", [2, SP, D])
    xcs_d = din("xcs", [SS, D])
    w_in0 = din("w_in0", [D, 3904])
    w_qb = din("w_qb", [512, 1536])
    w_kvb = din("w_kvb", [256, 2048])
    w_o0 = din("w_o0", [D, D])
    w_pool = din("w_pool", [4, 512, 512])
    w_up = din("w_up", [2, D, 8192])
    w_down = din("w_down", [2, 8192, D])
    const_d = din("consts", [128, NCONST])
    ident_d = din("ident", [128, 128])
    ropec_d = din("ropec", [64, 2, SS])
    ropeq_d = din("ropeq", [NT, 64, 2, W])
    mask_d = din("mask", [NT, W])
    invt_d = din("invt", [NT, 4, W])
    y_d = nc.dram_tensor("y", [NT, TOWN, D], F32, kind="ExternalOutput").ap()

    S = Sched()
    with ExitStack() as es:
        def sb(name, shape, dt):
            return es.enter_context(nc.sbuf_tensor(name, shape, dt))
        xT = sb("xT", [128, NCH, W], F32)
        hT = sb("hT", [128, NCH * W], BF16)
        mh = sb("mh", [128, NCH, W], BF16)
        cqg = sb("cqg", [128, 4, W], BF16)
        ckv = sb("ckv", [128, 2, SS], BF16)
        KR = sb("KR", [128, SS], BF16)
        sskr = sb("sskr", [128, SS // 128], F32)
        rk = sb("rk", [128, 32], F32)
        sqK2 = sb("sqK2", [128, 512], BF16)
        stin = sb("stin", [128, D], F32)
        stout = sb("stout", [128, D], F32)
        ubuf = sb("ubuf", [128, W + 2], F32)
        T = [sb(f"T{i}", [128, W], F32) for i in range(6)]
        sq = [sb(f"sq{i}", [128, W], BF16) for i in range(2)]
        ropet = sb("ropet", [64, 2, W], F32)
        wqb = sb("wqb", [128, 4, 1600], BF16)
        ones_lo = sb("ones_lo", [128, 128], BF16)
        wkvb = sb("wkvb", [128, 2, 2048], BF16)
        slots = [sb(f"ws{i}", [128, SLOT_ELEMS], BF16) for i in range(NSLOT)]
        cst = sb("cst", [128, NCONST], F32)
        ident = sb("ident_sb", [128, 128], F32)
        ones = sb("ones", [128, 128], BF16)
        onesf = sb("onesf", [128, 128], F32)
        ps = [es.enter_context(nc.psum_tensor(f"ps{i}", [128, 2, 512], F32)) for i in range(4)]
        esem = {e: es.enter_context(nc.semaphore(f"s_{e}")) for e in ("pe", "act", "dve", "gq")}
        dnames = [f"w{i}" for i in range(NSLOT)] + ["stin", "stout", "cst", "tab", "res0", "res1", "idn", "msk", "inv0", "inv1"]
        dsem = {n: es.enter_context(nc.semaphore(f"d_{n}")) for n in dnames}
        block = es.enter_context(nc.Block())

        def hTc(c, wd=W):
            return hT[:, c * W:c * W + wd]
        Kseg = hT[:, 0:2048]
        Vseg = hT[:, 4 * W:4 * W + 2048].rearrange("p (c n) -> p c n", c=16)
        sqK = hT[:, 8 * W:8 * W + 512]
        Qn = hT[:, 9 * W:10 * W]
        Qr = hT[0:64, 10 * W:11 * W]
        Pt = [hT[:, (11 + i) * W:(12 + i) * W] for i in range(3)]
        K_KSEG = [("hT", c) for c in range(4)]
        K_VSEG = [("hT", c) for c in range(4, 8)]
        K_SQK = [("hT", 8)]
        K_Q = [("hT", 9), ("hT", 10)]
        K_PT = [[("hT", 11 + i)] for i in range(3)]

        def pk(p):
            return [("ps", 0, 0), ("ps", 0, "v")] if p == 0 else [("ps", p)]

        def b2(ap, wd):
            return ap.rearrange("p (b n) -> p b n", b=2)

        def cc(i, n=1):
            return cst[:, i:i + n]

        S.op("sync", lambda e: e.dma_start(out=cst[:], in_=const_d), w=[("cst",)], dma="cst")
        S.op("sync", lambda e: e.dma_start(out=ident[:], in_=ident_d), w=[("ident",)], dma="idn")
        S.op("dve", lambda e: e.memset(wqb[:, :, 1536:1600], 0.0), w=[("wqbpad",)])
        S.op("gq", lambda e: e.dma_start(out=wqb[:, :, 0:1536], in_=w_qb.rearrange("(c p) n -> p c n", p=128)), w=[("wqb",)], dma="res0")
        S.op("gq", lambda e: e.dma_start(out=wkvb[:], in_=w_kvb.rearrange("(c p) n -> p c n", p=128)), w=[("wkvb",)], dma="res1")

        def init_dve(e):
            e.memset(ones[:], 1.0)
            e.memset(ones_lo[:], 1.0)
            e.memset(ones_lo[64:128, :], 0.0)
            e.memset(onesf[:], 1.0)
            e.memset(hT[:], 0.0)
            for j in range(SS // 512):
                e.memset(KR[64:128, j * 512:(j + 1) * 512], 0.0)
            return e.memset(ubuf[:], 0.0)
        S.op("dve", init_dve, w=[("ones",), ("ubuf",)] + [("hT", c) for c in range(NCH)] + [("KR", kb) for kb in range(SS // 512)])

        wstate = {"n": 0}

        def wload(src_ap, kc, ncols):
            i = wstate["n"] % NSLOT
            wstate["n"] += 1
            view = slots[i][:, 0:kc * ncols].rearrange("p (c n) -> p c n", c=kc)
            S.op("gq", lambda e: e.dma_start(out=view, in_=src_ap), w=[("ws", i)], dma=f"w{i}")
            return view, ("ws", i)

        pstate = {"n": 0}

        def next_pair():
            p = pstate["n"] % 4
            pstate["n"] += 1
            return p

        evs = {"n": 0}

        def ev_eng():
            evs["n"] += 1
            return "act" if evs["n"] % 2 else "dve"

        def copy_op(eng, out, in_, r, w):
            if eng == "act":
                S.op("act", lambda e: e.activation(out=out, in_=in_, func=AF.Copy), r=r, w=w)
            else:
                S.op("dve", lambda e: e.tensor_copy(out=out, in_=in_), r=r, w=w)

        def mm(pair, lhs_list, rhs_list, wd, r, M=128, keys=None):
            bw = wd // 2
            nk = len(lhs_list)

            def fn(e):
                ins = None
                for k in range(nk):
                    for b in range(2):
                        ins = e.matmul(ps[pair][0:M, b, 0:bw], lhsT=lhs_list[k], rhs=rhs_list[k][:, b * bw:(b + 1) * bw],
                                       start=(k == 0), stop=(k == nk - 1))
                return ins
            S.op("pe", fn, r=r, w=keys if keys is not None else pk(pair))

        def psv(pair, wd, M=128):
            return ps[pair][0:M, :, 0:wd // 2]

        def load_x(src, wd):
            r0 = 0
            gi = 0
            while r0 < wd:
                R = min(128, wd - r0)
                if gi % 2 == 0:
                    stg, skeys, sdma = stin, [("stin",)], "stin"
                else:
                    stg, skeys, sdma = stout, [("stout", 0), ("stout", 1)], "stout"
                S.op("sync", lambda e, r0=r0, R=R, stg=stg: e.dma_start(out=stg[0:R, :], in_=src[r0:r0 + R, :]), w=skeys, dma=sdma)
                for half in range(2):
                    p = next_pair()
                    pv = ps[p][:].rearrange("p b (j n) -> p (b j) n", n=128)

                    def fn(e, R=R, half=half, pv=pv, stg=stg):
                        ins = None
                        for j in range(8):
                            c = half * 8 + j
                            ins = e.transpose(out=pv[:, j, 0:R], in_=stg[0:R, c * 128:(c + 1) * 128], identity=ident[0:R, 0:R])
                        return ins
                    S.op("pe", fn, r=skeys + [("ident",)], w=pk(p))
                    copy_op(ev_eng(), xT[:, half * 8:half * 8 + 8, r0:r0 + R], pv[:, 0:8, 0:R], r=pk(p),
                            w=[("xT", c) for c in range(half * 8, half * 8 + 8)])
                r0 += R
                gi += 1

        def rstd_from_ms(pair, wd, scale, dst, extra_r=()):
            def fn(e):
                e.activation(out=b2(dst[:, 0:wd], wd), in_=psv(pair, wd), func=AF.Ln, bias=EPS, scale=scale)
                return e.activation(out=dst[:, 0:wd], in_=dst[:, 0:wd], func=AF.Exp, scale=-0.5)
            S.op("act", fn, r=pk(pair) + list(extra_r), w=[("T", id(dst))])

        def tk(t):
            return ("T", id(t))

        def norm_stats(wd, dst):
            p = next_pair()
            bw = wd // 2
            for c in range(NCH):
                s_ = sq[c % 2]
                if c % 2 == 0:
                    S.op("act", lambda e, c=c, s_=s_: e.activation(out=s_[:, 0:wd], in_=xT[:, c, 0:wd], func=AF.Square),
                         r=[("xT", c)], w=[("sq", c % 2)])
                else:
                    S.op("dve", lambda e, c=c, s_=s_: e.tensor_tensor(out=s_[:, 0:wd], in0=xT[:, c, 0:wd], in1=xT[:, c, 0:wd], op=ALU.mult),
                         r=[("xT", c)], w=[("sq", c % 2)])

                def fn(e, c=c, s_=s_):
                    ins = None
                    for b in range(2):
                        ins = e.matmul(ps[p][:, b, 0:bw], lhsT=ones[:], rhs=s_[:, b * bw:(b + 1) * bw], start=(c == 0), stop=(c == NCH - 1))
                    return ins
                S.op("pe", fn, r=[("sq", c % 2), ("ones",)], w=pk(p))
            rstd_from_ms(p, wd, 1.0 / D, dst)

        def norm_to_hT(gcol, wd):
            rstd = T[5]
            norm_stats(wd, rstd)
            for c in range(NCH):
                S.op("dve", lambda e, c=c: e.scalar_tensor_tensor(out=hTc(c, wd), in0=xT[:, c, 0:wd], scalar=cc(gcol + c), in1=rstd[:, 0:wd],
                                                                  op0=ALU.mult, op1=ALU.mult),
                     r=[("xT", c), tk(rstd), ("cst",)], w=[("hT", c)])

        def rope(a, r1, r2, out, wd, rkeys, wkeys):
            cos = ropet[:, 0, 0:wd]
            sin = ropet[:, 1, 0:wd]

            def fn(e):
                e.tensor_tensor(out=r1[0:64, 0:wd], in0=a[0:64, 0:wd], in1=cos[0:64, :], op=ALU.mult)
                e.tensor_tensor(out=r2[0:32, 0:wd], in0=a[32:64, 0:wd], in1=sin[32:64, :], op=ALU.mult)
                e.tensor_tensor(out=r2[32:64, 0:wd], in0=a[0:32, 0:wd], in1=sin[0:32, :], op=ALU.mult)
                e.tensor_tensor(out=out[0:32, 0:wd], in0=r1[0:32, 0:wd], in1=r2[0:32, 0:wd], op=ALU.subtract)
                return e.tensor_tensor(out=out[32:64, 0:wd], in0=r1[32:64, 0:wd], in1=r2[32:64, 0:wd], op=ALU.add)
            S.op("dve", fn, r=[tk(a), ("ropet",)] + list(rkeys), w=[tk(r1), tk(r2)] + list(wkeys))

        def ctx_prep(src, Sk):
            nblk = Sk // 512
            for kb in range(nblk):
                wd = 512
                load_x(src[kb * 512:(kb + 1) * 512, :], wd)
                norm_to_hT(C_NM0, wd)
                S.op("sync", lambda e, kb=kb: e.dma_start(out=ropet[:, :, 0:512], in_=ropec_d[:, :, kb * 512:(kb + 1) * 512]),
                     w=[("ropet",)], dma="tab")
                wa, ka = wload(w_in0[:, 3584:3840].rearrange("(c p) n -> p c n", p=128), 16, 256)
                wb, kb_ = wload(w_in0[:, 3840:3904].rearrange("(c p) n -> p c n", p=128), 16, 64)
                hk = [("hT", c) for c in range(NCH)]
                rh = [hTc(c, wd) for c in range(NCH)]
                pm = [2, 3]
                for m in range(2):
                    mm(pm[m], [wa[:, k, m * 128:(m + 1) * 128] for k in range(NCH)], rh, wd, r=hk + [ka])
                    S.op("act", lambda e, m=m: e.activation(out=b2(sq[m][:, 0:wd], wd), in_=psv(pm[m], wd), func=AF.Square),
                         r=pk(pm[m]), w=[("sq", m)])

                def fn(e):
                    ins = None
                    for m in range(2):
                        for b in range(2):
                            ins = e.matmul(ps[1][:, b, 0:256], lhsT=ones[:], rhs=sq[m][:, b * 256:(b + 1) * 256], start=(m == 0), stop=(m == 1))
                    return ins
                S.op("pe", fn, r=[("sq", 0), ("sq", 1), ("ones",)], w=pk(1))
                rstd_from_ms(1, wd, 1.0 / 256, T[5])
                for m in range(2):
                    S.op("dve", lambda e, m=m, kb=kb: e.scalar_tensor_tensor(out=b2(ckv[:, m, kb * 512:(kb + 1) * 512], wd), in0=psv(pm[m], wd),
                                                                             scalar=cc(C_GKVA + m), in1=b2(T[5][:, 0:wd], wd), op0=ALU.mult, op1=ALU.mult),
                         r=pk(pm[m]) + [tk(T[5]), ("cst",)], w=[("ckv", kb)])
                mm(0, [wb[:, k, 0:64] for k in range(NCH)], rh, wd, r=hk + [kb_], M=64)
                S.op("act", lambda e: e.activation(out=b2(sq[0][0:64, 0:wd], wd), in_=psv(0, wd, 64), func=AF.Square), r=pk(0), w=[("sq", 0)])
                S.op("dve", lambda e: e.tensor_scalar(out=b2(T[4][0:64, 0:wd], wd), in0=psv(0, wd, 64), scalar1=cst[0:64, C_GKR:C_GKR + 1], scalar2=None, op0=ALU.mult),
                     r=pk(0) + [("cst",)], w=[tk(T[4])])

                def fn2(e):
                    ins = None
                    for j in range(4):
                        ins = e.matmul(ps[1][:, 1, 256 + j:257 + j], lhsT=sq[0][0:64, j * 128:(j + 1) * 128], rhs=ones[0:64, 0:1], start=True, stop=True)
                    return ins
                S.op("pe", fn2, r=[("sq", 0), ("ones",)], w=pk(1))
                S.op("dve", lambda e, kb=kb: e.tensor_copy(out=sskr[:, kb * 4:(kb + 1) * 4], in_=ps[1][:, 1, 256:260]), r=pk(1), w=[("sskr", kb)])
                rope(T[4], T[3], T[2], KR[0:64, kb * 512:(kb + 1) * 512], wd, [], [("KR", kb)])

        def in_proj_conv():
            hk = [("hT", c) for c in range(NCH)]
            rh = [hTc(c) for c in range(NCH)]
            for j in range(8):
                views = []
                for base in (1024, 2048, 0):
                    c0 = base + j * 128
                    views.append(wload(w_in0[:, c0:c0 + 128].rearrange("(c p) n -> p c n", p=128), 16, 128))
                (wxc, kxc), (wxi, kxi), (wxb, kxb) = views
                p1, p2, p3 = next_pair(), next_pair(), next_pair()
                mm(p1, [wxc[:, k, :] for k in range(NCH)], rh, W, r=hk + [kxc])
                S.op("act", lambda e, p1=p1: e.activation(out=b2(T[1][:], W), in_=psv(p1, W), func=AF.Copy), r=pk(p1), w=[tk(T[1])])
                mm(p2, [wxi[:, k, :] for k in range(NCH)], rh, W, r=hk + [kxi])
                S.op("act", lambda e, p2=p2: e.activation(out=b2(ubuf[:, 1:W + 1], W), in_=psv(p2, W), func=AF.Copy), r=pk(p2), w=[("ubuf",)])
                mm(p3, [wxb[:, k, :] for k in range(NCH)], rh, W, r=hk + [kxb])

                def fn(e, j=j, p3=p3):
                    cw = C_CW + 3 * j
                    e.tensor_tensor(out=ubuf[:, 1:W + 1], in0=T[1][:], in1=ubuf[:, 1:W + 1], op=ALU.mult)
                    e.tensor_scalar(out=T[3][:], in0=ubuf[:, 1:W + 1], scalar1=cc(cw + 1), scalar2=None, op0=ALU.mult)
                    e.scalar_tensor_tensor(out=T[3][:], in0=ubuf[:, 0:W], scalar=cc(cw), in1=T[3][:], op0=ALU.mult, op1=ALU.add)
                    e.scalar_tensor_tensor(out=T[3][:], in0=ubuf[:, 2:W + 2], scalar=cc(cw + 2), in1=T[3][:], op0=ALU.mult, op1=ALU.add)
                    return e.tensor_tensor(out=b2(mh[:, j, :], W), in0=psv(p3, W), in1=b2(T[3][:], W), op=ALU.mult)
                S.op("dve", fn, r=[tk(T[1]), ("ubuf",), ("cst",)] + pk(p3), w=[("ubuf",), tk(T[3]), ("mh", j)])
            pms = next_pair()
            for half in range(2):
                wq, kq = wload(w_in0[:, 3072 + half * 256:3072 + (half + 1) * 256].rearrange("(c p) n -> p c n", p=128), 16, 256)
                for mi in range(2):
                    m = half * 2 + mi
                    p = next_pair()
                    if p == pms:
                        p = next_pair()
                    mm(p, [wq[:, k, mi * 128:(mi + 1) * 128] for k in range(NCH)], rh, W, r=hk + [kq])
                    S.op("act", lambda e, p=p, m=m: e.activation(out=b2(sq[m % 2][:], W), in_=psv(p, W), func=AF.Square), r=pk(p), w=[("sq", m % 2)])
                    S.op("dve", lambda e, p=p, m=m: e.tensor_scalar(out=b2(cqg[:, m, :], W), in0=psv(p, W), scalar1=cc(C_GQA + m), scalar2=None, op0=ALU.mult),
                         r=pk(p) + [("cst",)], w=[("cqg", m)])

                    def fn(e, m=m):
                        ins = None
                        for b in range(2):
                            ins = e.matmul(ps[pms][:, b, 0:W // 2], lhsT=ones[:], rhs=sq[m % 2][:, b * (W // 2):(b + 1) * (W // 2)], start=(m == 0), stop=(m == 3))
                        return ins
                    S.op("pe", fn, r=[("sq", m % 2), ("ones",)], w=pk(pms))
            rstd_from_ms(pms, W, 1.0 / 512, T[2])

        def attention(Sk):
            bw = W // 2
            nseg = Sk // 2048
            scq = T[2]
            acc, rden, f, a_, r2 = T[0], T[1], T[3], T[4], T[5]
            stb = stout[:].bitcast(BF16)
            KB = [Kseg, stb[:, 0:2048]]
            VB = [Vseg, stb[:, 2048:4096].rearrange("p (c n) -> p c n", c=16)]
            kKB = [K_KSEG, [("stout", 0)]]
            kVB = [K_VSEG, [("stout", 1)]]
            SQB = [sqK, sqK2[:, :]]
            kSQB = [K_SQK, [("sqK2",)]]
            RKB = [rk[:, 0:16], rk[:, 16:32]]
            kRKB = [[("rk", 0)], [("rk", 1)]]
            QNB = [Qn, hT[:, 14 * W:15 * W]]
            QRB = [Qr, hT[0:64, 15 * W:16 * W]]
            QRF = [hT[:, 10 * W:11 * W], hT[:, 15 * W:16 * W]]

            def zq(e):
                e.memset(hT[64:128, 10 * W:11 * W], 0.0)
                return e.memset(hT[64:128, 15 * W:16 * W], 0.0)
            S.op("dve", zq, w=[("hT", 10), ("hT", 15)])
            kQB = [K_Q, [("hT", 14), ("hT", 15)]]
            units = [(h, sg) for h in range(8) for sg in range(nseg)]
            P0 = [("ps", 0, 0), ("ps", 0, "v")]

            def prep(u):
                h, sg = units[u]
                par = u % 2
                k0 = sg * 2048
                Kb, Vb, sqb, rkb = KB[par], VB[par], SQB[par], RKB[par]
                kK, kV, kS, kR = kKB[par], kVB[par], kSQB[par], kRKB[par]
                BK = [("ps", 0, 0), ("ps", 0, "v")]
                S.op("dve", lambda e: e.memset(rkb, 0.0), w=kR)
                for kb in range(4):
                    gkb = (k0 + kb * 512) // 512
                    bank = kb % 2

                    def fnk(e, kb=kb, bank=bank):
                        ins = None
                        for r in range(2):
                            ins = e.matmul(ps[0][:, bank, 0:512], lhsT=wkvb[:, r, h * 256:h * 256 + 128], rhs=ckv[:, r, k0 + kb * 512:k0 + (kb + 1) * 512],
                                           start=(r == 0), stop=(r == 1))
                        return ins
                    S.op("pe", fnk, r=[("wkvb",), ("ckv", gkb)], w=[BK[bank]])
                    S.op("dve", lambda e, kb=kb, bank=bank: e.tensor_scalar(out=Kb[:, kb * 512:(kb + 1) * 512], in0=ps[0][:, bank, 0:512], scalar1=cc(C_GKN), scalar2=None, op0=ALU.mult),
                         r=[BK[bank], ("cst",)], w=kK)
                yield
                for kc2 in range(8):
                    gkb = (k0 + kc2 * 256) // 512
                    bank = kc2 % 2

                    def fnv(e, kc2=kc2, bank=bank):
                        ins = None
                        for j in range(2):
                            ks = k0 + (kc2 * 2 + j) * 128
                            for r in range(2):
                                ins = e.matmul(ps[0][:, bank, j * 256:(j + 1) * 256], lhsT=ckv[:, r, ks:ks + 128], rhs=wkvb[:, r, h * 256:h * 256 + 256],
                                               start=(r == 0), stop=(r == 1))
                        return ins
                    S.op("pe", fnv, r=[("wkvb",), ("ckv", gkb)], w=[BK[bank]])

                    def fna(e, kc2=kc2, bank=bank):
                        ins = None
                        for j in range(2):
                            kc = kc2 * 2 + j
                            ins = e.activation(out=sqb[:, j * 128:(j + 1) * 128], in_=ps[0][:, bank, j * 256:j * 256 + 128], func=AF.Square,
                                               accum_out=rkb[:, kc:kc + 1])
                        return ins
                    S.op("act", fna, r=[BK[bank]] + kR, w=kS + kR)
                    S.op("dve", lambda e, kc2=kc2, bank=bank: e.tensor_copy(out=Vb[:, kc2 * 2:kc2 * 2 + 2, :],
                                                                            in_=ps[0][:, bank, :].rearrange("p (c n) -> p c n", c=2)[:, :, 128:256]),
                         r=[BK[bank]], w=kV)
                yield
                S.op("dve", lambda e: e.tensor_tensor(out=rkb, in0=rkb, in1=sskr[:, sg * 16:(sg + 1) * 16], op=ALU.add),
                     r=kR + [("sskr", sg * 4 + i) for i in range(4)], w=kR)
                S.op("act", lambda e: e.activation(out=rkb, in_=rkb, func=AF.Ln, bias=192.0 * EPS, scale=1.0), r=kR, w=kR)
                S.op("dve", lambda e: e.tensor_scalar(out=rkb, in0=rkb, scalar1=-0.5, scalar2=None, op0=ALU.mult), r=kR, w=kR)
                S.op("act", lambda e: e.activation(out=rkb, in_=rkb, func=AF.Exp), r=kR, w=kR)
                yield
                if sg != 0:
                    return
                qp = h % 2
                Qn_, Qr_, kQ = QNB[qp], QRB[qp], kQB[qp]
                rq = [cqg[:, r, :] for r in range(4)]
                kq = [("cqg", r) for r in range(4)] + [("wqb",)]
                mm(0, [wqb[:, r, h * 192:h * 192 + 128] for r in range(4)], rq, W, r=kq, keys=P0)
                S.op("act", lambda e: e.activation(out=b2(sq[0][:], W), in_=psv(0, W), func=AF.Square), r=P0, w=[("sq", 0)])
                S.op("dve", lambda e: e.tensor_copy(out=b2(ubuf[:, 0:W], W), in_=psv(0, W)), r=P0, w=[("ubuf",)])
                yield
                mm(0, [wqb[:, r, h * 192 + 128:h * 192 + 256] for r in range(4)], rq, W, r=kq + [("wqbpad",)], keys=P0)
                S.op("act", lambda e: e.activation(out=b2(sq[1][:], W), in_=psv(0, W), func=AF.Square), r=P0, w=[("sq", 1)])
                S.op("dve", lambda e: e.tensor_scalar(out=b2(a_[0:64, :], W), in0=psv(0, W, 64), scalar1=cst[0:64, C_GQR:C_GQR + 1], scalar2=None, op0=ALU.mult),
                     r=P0 + [("cst",)], w=[tk(a_)])
                yield

                def fnm(e):
                    ins = None
                    for b in range(2):
                        e.matmul(ps[0][:, b, 0:bw], lhsT=ones[:], rhs=sq[0][:, b * bw:(b + 1) * bw], start=True, stop=False)
                        ins = e.matmul(ps[0][:, b, 0:bw], lhsT=ones_lo[:], rhs=sq[1][:, b * bw:(b + 1) * bw], start=False, stop=True)
                    return ins
                S.op("pe", fnm, r=[("sq", 0), ("sq", 1), ("ones",)], w=P0)

                def fnf(e):
                    e.tensor_tensor(out=b2(f[:], W), in0=psv(0, W), in1=b2(scq[:], W), op=ALU.mult)
                    return e.tensor_tensor(out=f[:], in0=f[:], in1=scq[:], op=ALU.mult)
                S.op("dve", fnf, r=P0 + [tk(scq)], w=[tk(f)])
                def fnl(e):
                    e.activation(out=f[:], in_=f[:], func=AF.Ln, bias=EPS, scale=1.0 / 192)
                    return e.activation(out=f[:], in_=f[:], func=AF.Exp, scale=-0.5)
                S.op("act", fnl, r=[tk(f)], w=[tk(f)])
                yield

                def fnq(e):
                    e.tensor_tensor(out=f[:], in0=f[:], in1=scq[:], op=ALU.mult)
                    e.scalar_tensor_tensor(out=Qn_, in0=ubuf[:, 0:W], scalar=cc(C_GQN), in1=f[:], op0=ALU.mult, op1=ALU.mult)
                    return e.tensor_tensor(out=a_[0:64, :], in0=a_[0:64, :], in1=f[0:64, :], op=ALU.mult)
                S.op("dve", fnq, r=[tk(f), tk(scq), tk(a_), ("ubuf",), ("cst",)], w=[tk(f), tk(a_)] + kQ)
                yield
                rope(a_, f, r2, Qr_, W, [], kQ)
                yield

            def drain(g, n=None):
                if g is None:
                    return
                k = 0
                for _ in g:
                    k += 1
                    if n is not None and k >= n:
                        return

            sc_i = {"n": 0}
            pt_i = {"n": 0}
            drain(prep(0))
            for u, (h, sg) in enumerate(units):
                par = u % 2
                qp = h % 2
                k0 = sg * 2048
                Kb, Vb, rkb = KB[par], VB[par], RKB[par]
                kK, kV, kR = kKB[par], kVB[par], kRKB[par]
                Qn_, Qr_, kQ = QNB[qp], QRF[qp], kQB[qp]
                nxt = prep(u + 1) if u + 1 < len(units) else None
                if not INTERLEAVE:
                    drain(nxt)
                    nxt = None
                prs = {}

                def scores(kc):
                    pr = SC_RING[sc_i["n"] % len(SC_RING)]
                    sc_i["n"] += 1
                    prs[kc] = pr
                    gkb = (k0 + kc * 128) // 512

                    def fn(e, kc=kc, pr=pr, Kb=Kb, Qn_=Qn_, Qr_=Qr_, k0=k0):
                        ins = None
                        for b in range(2):
                            e.matmul(ps[pr][:, b, 0:bw], lhsT=Kb[:, kc * 128:(kc + 1) * 128], rhs=Qn_[:, b * bw:(b + 1) * bw], start=True, stop=False)
                            ins = e.matmul(ps[pr][:, b, 0:bw], lhsT=KR[:, k0 + kc * 128:k0 + (kc + 1) * 128], rhs=Qr_[:, b * bw:(b + 1) * bw], start=False, stop=True)
                        return ins
                    S.op("pe", fn, r=kK + kQ + [("KR", gkb)], w=pk(pr))
                depth = len(SC_RING) - 1
                for k_ in range(depth):
                    scores(k_)
                for kc in range(16):
                    if kc + depth < 16:
                        scores(kc + depth)
                    pr = prs[kc]
                    pi = pt_i["n"] % 3
                    pt_i["n"] += 1
                    ptile = Pt[pi]
                    S.op("act", lambda e, kc=kc, pr=pr, ptile=ptile, rkb=rkb: e.activation(out=b2(ptile, W), in_=psv(pr, W), func=AF.Exp, scale=rkb[:, kc:kc + 1]),
                         r=pk(pr) + kR, w=K_PT[pi])
                    first = (sg == 0 and kc == 0)
                    last = (sg == nseg - 1 and kc == 15)

                    def fnpv(e, kc=kc, ptile=ptile, first=first, last=last, Vb=Vb):
                        ins = None
                        for b in range(2):
                            ins = e.matmul(ps[1][:, b, 0:bw], lhsT=Vb[:, kc, :], rhs=ptile[:, b * bw:(b + 1) * bw], start=first, stop=last)
                        return ins
                    S.op("pe", fnpv, r=kV + K_PT[pi], w=pk(1))
                    if first:
                        S.op("dve", lambda e, ptile=ptile: e.tensor_copy(out=acc[:], in_=ptile), r=K_PT[pi], w=[tk(acc)])
                    else:
                        S.op("dve", lambda e, ptile=ptile: e.tensor_tensor(out=acc[:], in0=acc[:], in1=ptile, op=ALU.add), r=K_PT[pi] + [tk(acc)], w=[tk(acc)])
                    if nxt is not None:
                        for _ in range(ILV_SCHED.get(kc, 0)):
                            next(nxt, None)
                drain(nxt)
                if sg == nseg - 1:
                    pd = SC_RING[sc_i["n"] % len(SC_RING)]
                    sc_i["n"] += 1

                    def fnd(e, pd=pd):
                        ins = None
                        for b in range(2):
                            ins = e.matmul(ps[pd][:, b, 0:bw], lhsT=onesf[:], rhs=acc[:, b * bw:(b + 1) * bw], start=True, stop=True)
                        return ins
                    S.op("pe", fnd, r=[tk(acc), ("ones",)], w=pk(pd))
                    def fnr_(e, pd=pd):
                        e.activation(out=b2(rden[:], W), in_=psv(pd, W), func=AF.Ln)
                        return e.activation(out=rden[:], in_=rden[:], func=AF.Exp, scale=-1.0)
                    S.op("act", fnr_, r=pk(pd), w=[tk(rden)])
                    S.op("dve", lambda e, h=h: e.tensor_tensor(out=b2(mh[:, 8 + h, :], W), in0=psv(1, W), in1=b2(rden[:], W), op=ALU.mult),
                         r=pk(1) + [tk(rden)], w=[("mh", 8 + h)])

        def resid_add(p, c):
            S.op("dve", lambda e: e.tensor_tensor(out=b2(xT[:, c, :], W), in0=psv(p, W), in1=b2(xT[:, c, :], W), op=ALU.add),
                 r=pk(p) + [("xT", c)], w=[("xT", c)])

        def out_proj():
            mk = [("mh", c) for c in range(NCH)]
            rm = [mh[:, c, :] for c in range(NCH)]
            for ch in range(8):
                wv, kk = wload(w_o0[:, ch * 256:(ch + 1) * 256].rearrange("(c p) n -> p c n", p=128), 16, 256)
                for mi in range(2):
                    p = next_pair()
                    mm(p, [wv[:, k, mi * 128:(mi + 1) * 128] for k in range(NCH)], rm, W, r=mk + [kk])
                    resid_add(p, ch * 2 + mi)

        def mlp(layer):
            norm_to_hT(C_NMLP0 if layer == 0 else C_NMLP1, W)
            hk = [("hT", c) for c in range(NCH)]
            rh = [hTc(c) for c in range(NCH)]

            def up(g, q4):
                hs = (g % 2) * 8
                f0 = g * 1024 + q4 * 256
                wv, kk = wload(w_up[layer, :, f0:f0 + 256].rearrange("(c p) n -> p c n", p=128), 16, 256)
                for mi in range(2):
                    p = next_pair()
                    fc = q4 * 2 + mi
                    mm(p, [wv[:, k, mi * 128:(mi + 1) * 128] for k in range(NCH)], rh, W, r=hk + [kk])
                    tt = T[fc % 2]
                    S.op("act", lambda e, p=p, tt=tt: e.activation(out=b2(tt[:], W), in_=psv(p, W), func=AF.Relu), r=pk(p), w=[tk(tt)])
                    S.op("dve", lambda e, tt=tt, fc=fc, hs=hs: e.tensor_tensor(out=mh[:, hs + fc, :], in0=tt[:], in1=tt[:], op=ALU.mult),
                         r=[tk(tt)], w=[("mh", hs + fc)])

            def down(g):
                hs = (g % 2) * 8
                mk = [("mh", hs + k) for k in range(8)]
                rm = [mh[:, hs + k, :] for k in range(8)]
                for dq in range(4):
                    wv, kk = wload(w_down[layer, g * 1024:(g + 1) * 1024, dq * 512:(dq + 1) * 512].rearrange("(c p) n -> p c n", p=128), 8, 512)
                    for dc in range(4):
                        p = next_pair()
                        mm(p, [wv[:, k, dc * 128:(dc + 1) * 128] for k in range(8)], rm, W, r=mk + [kk])
                        resid_add(p, dq * 4 + dc)
            for q4 in range(4):
                up(0, q4)
            for g in range(8):
                if g + 1 < 8:
                    up(g + 1, 0)
                down(g)
                if g + 1 < 8:
                    for q4 in range(1, 4):
                        up(g + 1, q4)

        def wsum(src, ta, tb, L, eng_fn_list):
            cur = src
            bufs = [ta, tb]
            lo, hi = 0, W
            for s in range(L):
                dst = bufs[s % 2]
                if s == 0:
                    nlo, nhi = lo + 1, hi
                    eng_fn_list.append(lambda e, cur=cur, dst=dst, nlo=nlo, nhi=nhi: e.tensor_tensor(
                        out=dst[:, nlo:nhi], in0=cur[:, nlo - 1:nhi - 1], in1=cur[:, nlo:nhi], op=ALU.add))
                else:
                    sh = 1 << (s - 1)
                    nlo, nhi = lo + sh, hi - sh
                    eng_fn_list.append(lambda e, cur=cur, dst=dst, nlo=nlo, nhi=nhi, sh=sh: e.tensor_tensor(
                        out=dst[:, nlo:nhi], in0=cur[:, nlo - sh:nhi - sh], in1=cur[:, nlo + sh:nhi + sh], op=ALU.add))
                lo, hi = nlo, nhi
                cur = dst
            return cur

        def pool_layer(ti):
            rstd = T[5]
            norm_stats(W, rstd)
            maskt = T[4]
            S.op("sync", lambda e: e.dma_start(out=maskt[:], in_=mask_d[ti:ti + 1, :].partition_broadcast(128)), w=[tk(maskt)], dma="msk")
            S.op("dve", lambda e: e.tensor_tensor(out=rstd[:], in0=rstd[:], in1=maskt[:], op=ALU.mult), r=[tk(rstd), tk(maskt)], w=[tk(rstd)])
            o0, o1 = HALO, HALO + TOWN
            wl = {}
            for gi in range(3):
                wl[gi] = wload(w_pool[gi].rearrange("(c p) n -> p c n", p=128), 4, 512)

            def make_inv(gi, ta, tb, inv, wkeys):
                S.op("sync", lambda e: e.dma_start(out=inv[:, 0:W], in_=invt_d[ti, gi:gi + 1, :].partition_broadcast(128)), w=wkeys, dma=f"inv{gi % 2}")

            def group_elementwise(gi, eng, h32, ta, tb, inv, tkeys):
                L = gi + 1
                for cq_ in range(4):
                    c = gi * 4 + cq_
                    if eng == "dve":
                        fl = [lambda e, c=c: e.scalar_tensor_tensor(out=h32[:, 0:W], in0=xT[:, c, :], scalar=cc(C_NM1 + c), in1=rstd[:], op0=ALU.mult, op1=ALU.mult)]
                    else:
                        fl = [lambda e, c=c: e.tensor_scalar(out=h32[:, 0:W], in0=xT[:, c, :], scalar1=cc(C_NM1 + c), scalar2=None, op0=ALU.mult),
                              lambda e: e.tensor_tensor(out=h32[:, 0:W], in0=h32[:, 0:W], in1=rstd[:], op=ALU.mult)]
                    s_ = wsum(h32, ta, tb, L, fl)
                    other = tb if s_ is ta else ta
                    fl.append(lambda e, s_=s_, other=other: e.tensor_tensor(out=other[:, o0:o1], in0=s_[:, o0:o1], in1=inv[:, o0:o1], op=ALU.mult))
                    fl.append(lambda e, c=c, other=other: e.tensor_tensor(out=hTc(c)[:, o0:o1], in0=other[:, o0:o1], in1=h32[:, o0:o1], op=ALU.subtract))

                    def run2(e, fl=fl):
                        ins = None
                        for f_ in fl:
                            ins = f_(e)
                        return ins
                    S.op(eng, run2, r=[("xT", c), tk(rstd), ("cst",)] + tkeys, w=tkeys + [("hT", c)])

            def group_matmul(gi, wv, kk):
                hk = [("hT", gi * 4 + k) for k in range(4)]
                rh = [hTc(gi * 4 + k) for k in range(4)]
                for m in range(4):
                    p = next_pair()
                    c = gi * 4 + m
                    mm(p, [wv[:, k, m * 128:(m + 1) * 128] for k in range(4)], rh, W, r=hk + [kk])
                    S.op("dve", lambda e, p=p, c=c: e.scalar_tensor_tensor(out=b2(xT[:, c, :], W), in0=psv(p, W), scalar=cc(C_PSC + c), in1=b2(xT[:, c, :], W),
                                                                           op0=ALU.mult, op1=ALU.add),
                         r=pk(p) + [("xT", c), ("cst",)], w=[("xT", c)])
            TK = [tk(T[0]), tk(T[1]), tk(T[2]), tk(T[3])]
            invb = [T[3], ubuf[:, 0:W]]
            invk = [[tk(T[3])], [("ubuf",)]]
            TK3 = [tk(T[0]), tk(T[1]), tk(T[2])]
            for gi in range(2):
                make_inv(gi, None, None, invb[gi % 2], invk[gi % 2])
            for gi in range(4):
                group_elementwise(gi, "dve", T[0], T[1], T[2], invb[gi % 2], TK3 + invk[gi % 2])
                if gi + 2 < 4:
                    make_inv(gi + 2, None, None, invb[gi % 2], invk[gi % 2])
                if gi < 3:
                    group_matmul(gi, *wl[gi])
            wv3, kk3 = wload(w_pool[3].rearrange("(c p) n -> p c n", p=128), 4, 512)
            group_matmul(3, wv3, kk3)

        stores = []

        def store(ti):
            for tg in range(4):
                col0 = HALO + tg * 128
                if tg % 2 == 0:
                    stg, skeys, sdma = stout, [("stout", 0), ("stout", 1)], "stout"
                else:
                    stg, skeys, sdma = stin, [("stin",), ("stin",)], "stin"
                for half in range(2):
                    p = next_pair()
                    pv = ps[p][:].rearrange("p b (j n) -> p (b j) n", n=128)

                    def fn(e, half=half, pv=pv, col0=col0):
                        ins = None
                        for j in range(8):
                            c = half * 8 + j
                            ins = e.transpose(out=pv[:, j, :], in_=xT[:, c, col0:col0 + 128], identity=ident[:, :])
                        return ins
                    S.op("pe", fn, r=[("xT", c) for c in range(half * 8, half * 8 + 8)] + [("ident",)], w=pk(p))
                    copy_op(ev_eng(), stg[:, half * 1024:(half + 1) * 1024].rearrange("p (j n) -> p j n", n=128), pv[:, 0:8, :], r=pk(p), w=[skeys[half]])
                o = S.op("sync", lambda e, tg=tg, stg=stg: e.dma_start(out=y_d[ti, tg * 128:(tg + 1) * 128, :], in_=stg[:, :]), r=skeys, dma=sdma)
                stores.append(o)

        ti = 0
        for ci, (kind, idx) in enumerate(ctxs):
            Sk = SP if kind == "p" else SS
            src = xcp_d[idx] if kind == "p" else xcs_d
            ctx_prep(src, Sk)
            while ti < NT and tiles[ti] == ci:
                load_x(xt_d[ti], W)
                S.op("sync", lambda e, ti=ti: e.dma_start(out=ropet[:, :, :], in_=ropeq_d[ti]), w=[("ropet",)], dma="tab")
                norm_to_hT(C_NM0, W)
                if stage == -2:
                    in_proj_conv()
                    attention(Sk)
                    S.op("dve", lambda e: e.tensor_copy(out=xT[:, 0, 12:28], in_=rk[:, :]), r=[("rk",)], w=[("xT", 0)])
                    S.op("dve", lambda e: e.tensor_copy(out=xT[:, 1, 12:28], in_=sskr[:, 0:16]), r=[("sskr", i) for i in range(4)], w=[("xT", 1)])
                if stage == -1:
                    in_proj_conv()
                    attention(Sk)
                    for c in range(NCH):
                        S.op("dve", lambda e, c=c: e.tensor_copy(out=xT[:, c, :], in_=mh[:, c, :]), r=[("mh", c)], w=[("xT", c)])
                if stage >= 1:
                    in_proj_conv()
                    attention(Sk)
                    out_proj()
                if stage >= 2:
                    mlp(0)
                if stage >= 3:
                    pool_layer(ti)
                if stage >= 4:
                    mlp(1)
                store(ti)
                ti += 1

        S.finalize()
        emap = {"pe": "tensor", "act": "scalar", "dve": "vector", "sync": "sync", "gq": "gpsimd"}

        @block.sync
        def _(e):
            S.emit("sync", e, esem, dsem, final_waits=stores[-2:])

        @block.gpsimd
        def _(e):
            S.emit("gq", e, esem, dsem)

        @block.tensor
        def _(e):
            S.emit("pe", e, esem, dsem)

        @block.scalar
        def _(e):
            S.emit("act", e, esem, dsem)

        @block.vector
        def _(e):
            S.emit("dve", e, esem, dsem)
    return nc


def _fm(g, nch):
    return np.ascontiguousarray(np.asarray(g, np.float32).reshape(nch, 128).T)


def _rope_table():
    inv = (1.0 / (np.float32(10000.0) ** (np.arange(0, 64, 2, dtype=np.float32) / np.float32(64)))).astype(np.float32)
    ang = np.arange(SS, dtype=np.float32)[:, None] * inv[None, :]
    cos = np.cos(ang).astype(np.float32).T
    sin = np.sin(ang).astype(np.float32).T
    tab = np.zeros((64, 2, SS), np.float32)
    tab[0:32, 0] = cos; tab[32:64, 0] = cos
    tab[0:32, 1] = sin; tab[32:64, 1] = sin
    return tab


TILES_FULL = [0, 0, 0, 0, 1, 1, 1, 1, 2, 2]
CTXS_FULL = [("p", 0), ("p", 1), ("s", 0)]


def make_inputs(core, tiles, ctxs, x_prompt, x_sample, common):
    NT = len(tiles)
    xt = np.zeros((NT, W, D), np.float32)
    ropeq = np.zeros((NT, 64, 2, W), np.float32)
    mask = np.zeros((NT, W), np.float32)
    invt = np.ones((NT, 4, W), np.float32)
    tab = common["ropec"]
    per_ctx = {}
    for t, ci in enumerate(tiles):
        k = per_ctx.get(ci, 0)
        per_ctx[ci] = k + 1
        kind, idx = ctxs[ci]
        if kind == "p":
            seq = x_prompt[2 * core + idx]
            Sk = SP
            start = k * TOWN
        else:
            seq = x_sample[0]
            Sk = SS
            start = core * 1024 + k * TOWN
        pos = np.arange(start - HALO, start + TOWN + HALO)
        valid = (pos >= 0) & (pos < Sk)
        pc = np.clip(pos, 0, Sk - 1)
        xt[t][valid] = seq[pos[valid]]
        ropeq[t] = tab[:, :, pc]
        mask[t] = valid.astype(np.float32)
        for gi, w_ in enumerate((2, 4, 8, 16)):
            lo_ = np.clip(pos - w_ // 2, 0, Sk)
            hi_ = np.clip(pos + w_ // 2, 0, Sk)
            invt[t, gi] = 1.0 / np.maximum(hi_ - lo_, 1).astype(np.float32)
    d = dict(common)
    d["xt"] = xt
    d["xcp"] = np.ascontiguousarray(x_prompt[2 * core:2 * core + 2])
    d["ropeq"] = ropeq
    d["mask"] = mask
    d["invt"] = invt
    return d


def make_common(x_sample, norm_mix0, w_in0, conv_w, q_a_norm, w_qb, kv_a_norm, w_kvb, q_norm, k_norm, w_o0,
                norm_mix1, w_pool, pool_scale, norm_mlp, w_up, w_down):
    f = lambda a: np.ascontiguousarray(np.asarray(a, np.float32))
    consts = np.zeros((128, NCONST), np.float32)
    consts[:, C_NM0:C_NM0 + 16] = _fm(norm_mix0[0], 16)
    consts[:, C_NM1:C_NM1 + 16] = _fm(norm_mix1[0], 16)
    consts[:, C_NMLP0:C_NMLP0 + 16] = _fm(norm_mlp[0], 16)
    consts[:, C_NMLP1:C_NMLP1 + 16] = _fm(norm_mlp[1], 16)
    consts[:, C_PSC:C_PSC + 16] = _fm(pool_scale[0], 16)
    consts[:, C_GQA:C_GQA + 4] = _fm(q_a_norm[0], 4)
    consts[:, C_GKVA:C_GKVA + 2] = _fm(kv_a_norm[0], 2)
    qn = np.asarray(q_norm[0], np.float32); kn = np.asarray(k_norm[0], np.float32)
    consts[:, C_GQN] = qn[0:128]
    consts[0:64, C_GQR] = qn[128:192]
    consts[:, C_GKN] = kn[0:128]
    consts[0:64, C_GKR] = kn[128:192]
    cw = np.asarray(conv_w[0], np.float32)
    consts[:, C_CW:C_CW + 24] = cw.reshape(3, 8, 128).transpose(2, 1, 0).reshape(128, 24)
    return {
        "xcs": f(x_sample[0]), "w_in0": f(w_in0[0]), "w_qb": f(w_qb[0]), "w_kvb": f(w_kvb[0]), "w_o0": f(w_o0[0]),
        "w_pool": f(w_pool[0]), "w_up": f(w_up), "w_down": f(w_down), "consts": consts,
        "ident": np.eye(128, dtype=np.float32), "ropec": _rope_table(),
    }


def kernel(x_prompt, x_sample, norm_mix0, w_in0, conv_w, q_a_norm, w_qb, kv_a_norm, w_kvb, q_norm, k_norm, w_o0,
           norm_mix1, w_pool, pool_scale, norm_mlp, w_up, w_down):
    x_prompt = np.asarray(x_prompt, np.float32)
    x_sample = np.asarray(x_sample, np.float32)
    common = make_common(x_sample, norm_mix0, w_in0, conv_w, q_a_norm, w_qb, kv_a_norm, w_kvb, q_norm, k_norm, w_o0,
                         norm_mix1, w_pool, pool_scale, norm_mlp, w_up, w_down)
    nc = build_program(TILES_FULL, CTXS_FULL)
    in_maps = [make_inputs(c, TILES_FULL, CTXS_FULL, x_prompt, x_sample, common) for c in range(8)]
    res = run_bass_kernel_spmd(nc, in_maps, core_ids=list(range(8)))
    y_prompt = np.zeros((16, SP, D), np.float32)
    y_sample = np.zeros((1, SS, D), np.float32)
    for c in range(8):
        y = np.asarray(res.results[c]["y"], np.float32)
        y_prompt[2 * c] = y[0:4].reshape(SP, D)
        y_prompt[2 * c + 1] = y[4:8].reshape(SP, D)
        y_sample[0, c * 1024:(c + 1) * 1024] = y[8:10].reshape(1024, D)
    return (y_prompt, y_sample)
```
